# Optimizing a Trainium2 kernel written in Bass

```python
import jax, jax.numpy as jnp
from jax import lax
import numpy as np

D_MODEL = 1024
BATCH = 8
SEQ = 2048
DEPTH = 4

GRID_W = 64
N_MEM = 256
HEAD_DIM = 64
N_BRANCH = 4
BRANCH_WIDTH = D_MODEL // N_BRANCH
IN_WIDTH = 9 * BRANCH_WIDTH
RET_HEADS = BRANCH_WIDTH // HEAD_DIM
RET_CHUNK = 128
ROPE_THETA = 10000.0
POOL_WINDOWS = (2, 4, 8, 16)
POOL_GROUPS = len(POOL_WINDOWS)
POOL_GROUP_DIM = BRANCH_WIDTH // POOL_GROUPS
NA_HEADS = BRANCH_WIDTH // HEAD_DIM
NA_WIN_ROWS = 8
NA_WIN_COLS = 16
NA_QBLOCK_COLS = 16
NA_KBLOCK_COLS = 2 * NA_QBLOCK_COLS
MEM_HEADS = BRANCH_WIDTH // HEAD_DIM
FF_HIDDEN = -(-8 * D_MODEL // (3 * 256)) * 256
NEG_INF = -1e30
EPS = 1e-6

kernel_name = "hybrid_retention_pool_natten_memory_encoder"


def rms_norm(x, g):
    xf = x.astype(jnp.float32)
    y = xf * lax.rsqrt(jnp.mean(xf * xf, axis=-1, keepdims=True) + EPS)
    return (y * g.astype(jnp.float32)).astype(x.dtype)


def split_heads(t, n_heads):
    b, s, _ = t.shape
    return t.reshape(b, s, n_heads, -1).transpose(0, 2, 1, 3)


def merge_heads(t):
    b, h, s, d = t.shape
    return t.transpose(0, 2, 1, 3).reshape(b, s, h * d)


def rotary(t, pos):
    half = t.shape[-1] // 2
    inv = ROPE_THETA ** (-jnp.arange(half, dtype=jnp.float32) / half)
    ang = pos[:, None] * inv[None, :]
    cos, sin = jnp.cos(ang), jnp.sin(ang)
    tf = t.astype(jnp.float32)
    t1, t2 = tf[..., :half], tf[..., half:]
    return jnp.concatenate([t1 * cos - t2 * sin, t1 * sin + t2 * cos], axis=-1).astype(t.dtype)


def retention_dir(q, k, v, log_gamma, include_diag):
    b, h, s, d = q.shape
    c = RET_CHUNK
    n = s // c
    dt = q.dtype
    qc, kc, vc = (t.reshape(b, h, n, c, d) for t in (q, k, v))
    idx = jnp.arange(c, dtype=jnp.float32)
    diff = idx[:, None] - idx[None, :]
    mask = (diff >= 0) if include_diag else (diff > 0)
    lg = log_gamma.astype(jnp.float32)[:, None]
    d_intra = jnp.where(mask[None], jnp.exp(jnp.where(mask, diff, 0.0)[None] * lg[:, :, None]), 0.0)
    scores = jnp.einsum('bhncd,bhnmd->bhncm', qc, kc) * d_intra[None, :, None].astype(dt)
    intra = jnp.einsum('bhncm,bhnme->bhnce', scores, vc)
    k_decay = jnp.exp((c - 1 - idx)[None, :] * lg).astype(dt)
    kv = jnp.einsum('bhncd,bhnce->nbhde', kc * k_decay[None, :, None, :, None], vc)
    chunk_decay = jnp.exp(c * lg[:, 0]).astype(dt)[None, :, None, None]

    def step(state, kv_n):
        return chunk_decay * state + kv_n, state

    _, states = lax.scan(step, jnp.zeros_like(kv[0]), kv)
    q_decay = jnp.exp((idx + 1)[None, :] * lg).astype(dt)
    cross = jnp.einsum('bhncd,nbhde->bhnce', qc * q_decay[None, :, None, :, None], states)
    return (intra + cross).reshape(b, h, s, d)


def bidirectional_retention(q, k, v, log_gamma_fwd, log_gamma_bwd):
    fwd = retention_dir(q, k, v, log_gamma_fwd, True)
    flip = lambda t: jnp.flip(t, axis=2)
    bwd = flip(retention_dir(flip(q), flip(k), flip(v), log_gamma_bwd, False))
    return fwd + bwd


def multiscale_pool(v, w_group, scale):
    b, s, cw = v.shape
    vg = v.reshape(b, s, POOL_GROUPS, POOL_GROUP_DIM)
    cs = jnp.cumsum(vg.astype(jnp.float32), axis=1)
    cs = jnp.concatenate([jnp.zeros_like(cs[:, :1]), cs], axis=1)
    t = np.arange(s)[:, None]
    half = np.array(POOL_WINDOWS)[None, :] // 2
    lo = np.clip(t - half, 0, s)
    hi = np.clip(t + half, 0, s)
    g_idx = np.arange(POOL_GROUPS)[None, :]
    win_sum = cs[:, hi, g_idx] - cs[:, lo, g_idx]
    count = jnp.asarray((hi - lo)[None, :, :, None], dtype=jnp.float32)
    pooled = (win_sum / count).astype(v.dtype) - vg
    mixed = jnp.einsum('bsgc,gce->bsge', pooled, w_group)
    return mixed.reshape(b, s, cw) * scale


def neighbourhood_attention(q, k, v, rpb):
    b, h, s, d = q.shape
    rows = s // GRID_W
    wr = min(NA_WIN_ROWS, rows)
    n_cb = GRID_W // NA_QBLOCK_COLS
    r = np.arange(rows)
    row_idx = np.clip(r - wr // 2, 0, rows - wr)[:, None] + np.arange(wr)[None, :]
    cb = np.arange(n_cb)
    kcol_idx = np.clip(cb * NA_QBLOCK_COLS - NA_WIN_COLS // 2, 0, GRID_W - NA_KBLOCK_COLS)[:, None] \
        + np.arange(NA_KBLOCK_COLS)[None, :]
    qcol = cb[:, None] * NA_QBLOCK_COLS + np.arange(NA_QBLOCK_COLS)[None, :]
    qwin = np.clip(qcol - NA_WIN_COLS // 2, 0, GRID_W - NA_WIN_COLS)
    col_mask = (kcol_idx[:, None, :] >= qwin[:, :, None]) & (kcol_idx[:, None, :] < qwin[:, :, None] + NA_WIN_COLS)
    row_off = row_idx - r[:, None]
    col_off = np.clip(kcol_idx[:, None, :] - qcol[:, :, None], -(NA_WIN_COLS - 1), NA_WIN_COLS - 1)
    bias = rpb[:, row_off[:, None, None, :, None] + NA_WIN_ROWS - 1,
               col_off[None, :, :, None, :] + NA_WIN_COLS - 1]
    bias = jnp.where(col_mask[None, None, :, :, None, :], bias.astype(jnp.float32), NEG_INF)

    qg = q.reshape(b, h, rows, n_cb, NA_QBLOCK_COLS, d)
    k_grid = k.reshape(b, h, rows, GRID_W, d)
    v_grid = v.reshape(b, h, rows, GRID_W, d)
    ri = row_idx[:, None, :, None]
    ci = kcol_idx[None, :, None, :]
    kg = k_grid[:, :, ri, ci]
    vg = v_grid[:, :, ri, ci]
    sc = jnp.einsum('bhrnqd,bhrnwkd->bhrnqwk', qg, kg).astype(jnp.float32) * (d ** -0.5) + bias[None]
    p = jax.nn.softmax(sc, axis=(-2, -1))
    o = jnp.einsum('bhrnqwk,bhrnwkd->bhrnqd', p.astype(v.dtype), vg)
    return o.reshape(b, h, s, d)


def memory_attention(q, mk, mv):
    sc = jnp.einsum('bhsd,bhmd->bhsm', q, mk).astype(jnp.float32) * (q.shape[-1] ** -0.5)
    p = jax.nn.softmax(sc, axis=-1)
    return jnp.einsum('bhsm,bhmd->bhsd', p.astype(mv.dtype), mv)


def hybrid_layer(x, mem, norm_mix_g, norm_mem_g, w_in, w_gate, ret_decay_fwd, ret_decay_bwd,
                 ret_norm_g, pool_w, pool_scale, na_q_norm_g, na_k_norm_g, na_rpb,
                 mem_q_norm_g, mem_k_norm_g, w_mem_kv, w_branch, w_out, norm_ffn_g,
                 w_ffn_in, w_ffn_out):
    b, s, dm = x.shape
    h = rms_norm(x, norm_mix_g)
    proj = h @ w_in
    rq, rk, rv, rg, pv, nq, nk, nv, mq = jnp.split(proj, 9, axis=-1)

    pos = jnp.arange(s, dtype=jnp.float32)
    rq_h = rotary(split_heads(rq, RET_HEADS), pos) * (HEAD_DIM ** -0.5)
    rk_h = rotary(split_heads(rk, RET_HEADS), pos)
    ret = bidirectional_retention(rq_h, rk_h, split_heads(rv, RET_HEADS),
                                  jax.nn.log_sigmoid(ret_decay_fwd.astype(jnp.float32)),
                                  jax.nn.log_sigmoid(ret_decay_bwd.astype(jnp.float32)))
    ret = merge_heads(rms_norm(ret, ret_norm_g.reshape(RET_HEADS, 1, HEAD_DIM))) * jax.nn.silu(rg)

    pool = multiscale_pool(pv, pool_w, pool_scale)

    na = merge_heads(neighbourhood_attention(rms_norm(split_heads(nq, NA_HEADS), na_q_norm_g),
                                             rms_norm(split_heads(nk, NA_HEADS), na_k_norm_g),
                                             split_heads(nv, NA_HEADS), na_rpb))

    mk, mv = jnp.split(rms_norm(mem, norm_mem_g) @ w_mem_kv, 2, axis=-1)
    mo = merge_heads(memory_attention(rms_norm(split_heads(mq, MEM_HEADS), mem_q_norm_g),
                                      rms_norm(split_heads(mk, MEM_HEADS), mem_k_norm_g),
                                      split_heads(mv, MEM_HEADS)))

    branches = jnp.stack([ret, pool, na, mo], axis=2)
    up = jnp.einsum('bsnc,ncd->bsnd', branches, w_branch)
    gates = jax.nn.sigmoid(h @ w_gate).reshape(b, s, N_BRANCH, dm)
    merged = jnp.einsum('bsnd,bsnd->bsd', gates, up)
    x = x + merged @ w_out

    a, g = jnp.split(rms_norm(x, norm_ffn_g) @ w_ffn_in, 2, axis=-1)
    return x + (jax.nn.silu(a) * g) @ w_ffn_out


def setup_inputs(seed: int = 0) -> dict:
    key = jax.random.key(seed)
    ks = jax.random.split(key, 22)
    f32 = jnp.float32
    L, D, BW = DEPTH, D_MODEL, BRANCH_WIDTH

    def nrm(k, shape, scale):
        return jax.random.normal(k, shape, f32) * scale

    base_logit = jnp.log(2.0 ** (5.0 + jnp.arange(RET_HEADS, dtype=f32)) - 1.0)
    return {
        "x": nrm(ks[0], (BATCH, SEQ, D), 1.0),
        "mem": nrm(ks[1], (BATCH, N_MEM, D), 1.0),
        "norm_mix_g": 1.0 + nrm(ks[2], (L, D), 0.02),
        "norm_mem_g": 1.0 + nrm(ks[3], (L, D), 0.02),
        "w_in": nrm(ks[4], (L, D, IN_WIDTH), D ** -0.5),
        "w_gate": nrm(ks[5], (L, D, N_BRANCH * D), D ** -0.5),
        "ret_decay_fwd": base_logit[None, :] + nrm(ks[6], (L, RET_HEADS), 0.1),
        "ret_decay_bwd": base_logit[None, :] + nrm(ks[7], (L, RET_HEADS), 0.1),
        "ret_norm_g": 1.0 + nrm(ks[8], (L, BW), 0.02),
        "pool_w": nrm(ks[9], (L, POOL_GROUPS, POOL_GROUP_DIM, POOL_GROUP_DIM), POOL_GROUP_DIM ** -0.5),
        "pool_scale": 1.0 + nrm(ks[10], (L, BW), 0.02),
        "na_q_norm_g": 1.0 + nrm(ks[11], (L, HEAD_DIM), 0.02),
        "na_k_norm_g": 1.0 + nrm(ks[12], (L, HEAD_DIM), 0.02),
        "na_rpb": nrm(ks[13], (L, NA_HEADS, 2 * NA_WIN_ROWS - 1, 2 * NA_WIN_COLS - 1), 0.02),
        "mem_q_norm_g": 1.0 + nrm(ks[14], (L, HEAD_DIM), 0.02),
        "mem_k_norm_g": 1.0 + nrm(ks[15], (L, HEAD_DIM), 0.02),
        "w_mem_kv": nrm(ks[16], (L, D, 2 * BW), D ** -0.5),
        "w_branch": nrm(ks[17], (L, N_BRANCH, BW, D), BW ** -0.5),
        "w_out": nrm(ks[18], (L, D, D), D ** -0.5),
        "norm_ffn_g": 1.0 + nrm(ks[19], (L, D), 0.02),
        "w_ffn_in": nrm(ks[20], (L, D, 2 * FF_HIDDEN), D ** -0.5),
        "w_ffn_out": nrm(ks[21], (L, FF_HIDDEN, D), FF_HIDDEN ** -0.5),
    }


def reference(x, mem, norm_mix_g, norm_mem_g, w_in, w_gate, ret_decay_fwd, ret_decay_bwd,
              ret_norm_g, pool_w, pool_scale, na_q_norm_g, na_k_norm_g, na_rpb,
              mem_q_norm_g, mem_k_norm_g, w_mem_kv, w_branch, w_out, norm_ffn_g,
              w_ffn_in, w_ffn_out):
    for l in range(DEPTH):
        x = hybrid_layer(x, mem, norm_mix_g[l], norm_mem_g[l], w_in[l], w_gate[l],
                         ret_decay_fwd[l], ret_decay_bwd[l], ret_norm_g[l], pool_w[l],
                         pool_scale[l], na_q_norm_g[l], na_k_norm_g[l], na_rpb[l],
                         mem_q_norm_g[l], mem_k_norm_g[l], w_mem_kv[l], w_branch[l],
                         w_out[l], norm_ffn_g[l], w_ffn_in[l], w_ffn_out[l])
    return x
```

```python
import contextlib
import math
import numpy as np
import concourse.bass as bass
import concourse.mybir as mybir
from concourse.bass_utils import run_bass_kernel_spmd

F32 = mybir.dt.float32
BF16 = mybir.dt.bfloat16
ALU = mybir.AluOpType
AF = mybir.ActivationFunctionType

NCORES = 8
S_LEN = 2048
D = 1024
KC = 8
TT = 512
NT = 4
NL = 4
FF = 2816
FCH = FF // 128
FHALF = FCH // 2
EPS = 1e-6
LN8 = math.log(0.125)
SAFE_RAW = True
PERM_ROT = True
STRICT_SAME_ENGINE = True
NSLOT = 4
SLOT_ELEMS = 2048

BK = 2048


class _Rec:
    __slots__ = ("p0", "p1", "b0", "b1", "kind", "tok", "dead")


def _ap_rects(ap):
    t = ap.tensor
    row = t.shape[1]
    dsz = mybir.dt.size(ap.dtype)
    pairs = ap.ap
    pstep, pn = pairs[0]
    off = ap.offset
    p0 = off // row
    foff = off - p0 * row
    dims = sorted((abs(s), n) for s, n in pairs[1:] if n > 1 and s != 0)
    starts = [foff]
    length = 1
    for i, (s, n) in enumerate(dims):
        if s == length:
            length *= n
        elif len(starts) * n <= 64:
            starts = [st + k * s for st in starts for k in range(n)]
        else:
            ext = sum((nn - 1) * ss for ss, nn in dims[i:])
            lo = min(starts)
            hi = max(starts) + length + ext
            return t.name, p0, p0 + pn, [(lo * dsz, hi * dsz)]
    return t.name, p0, p0 + pn, [(st * dsz, (st + length) * dsz) for st in starts]


class Tracker:
    def __init__(self):
        self.idx = {}
        self.rkey = {}

    def touch(self, arena, p0, p1, runs, kind, deps):
        idx = self.idx
        for (b0, b1) in runs:
            for bk in range(b0 // BK, (b1 - 1) // BK + 1):
                lst = idx.get((arena, bk))
                if not lst:
                    continue
                keep = []
                for r in lst:
                    if r.dead:
                        continue
                    if r.p1 <= p0 or r.p0 >= p1 or r.b1 <= b0 or r.b0 >= b1:
                        keep.append(r)
                        continue
                    if kind == "r" and r.kind == "r":
                        keep.append(r)
                        continue
                    for k, v in r.tok.items():
                        if deps.get(k, 0) < v:
                            deps[k] = v
                    if kind == "w" and r.p0 >= p0 and r.p1 <= p1 and r.b0 >= b0 and r.b1 <= b1:
                        r.dead = True
                        if r.kind == "r":
                            self.rkey.pop((arena, r.p0, r.p1, r.b0, r.b1), None)
                        continue
                    keep.append(r)
                idx[(arena, bk)] = keep

    def add(self, arena, p0, p1, runs, kind, tok):
        k, v = tok
        for (b0, b1) in runs:
            if kind == "r":
                key = (arena, p0, p1, b0, b1)
                r = self.rkey.get(key)
                if r is not None and not r.dead:
                    if r.tok.get(k, 0) < v:
                        r.tok[k] = v
                    continue
            r = _Rec()
            r.p0, r.p1, r.b0, r.b1, r.kind, r.tok, r.dead = p0, p1, b0, b1, kind, {k: v}, False
            if kind == "r":
                self.rkey[(arena, p0, p1, b0, b1)] = r
            for bk in range(b0 // BK, (b1 - 1) // BK + 1):
                self.idx.setdefault((arena, bk), []).append(r)


def _is_ap(a):
    return hasattr(a, "tensor") and hasattr(a, "ap") and hasattr(a, "offset")


def _on_chip(a):
    return "DRam" not in type(a.tensor).__name__


class _EngProxy:
    def __init__(self, sched, e):
        self._s = sched
        self._e = e

    def __getattr__(self, name):
        s, e = self._s, self._e

        def call(*args, **kw):
            outs, ins = [], []
            for i, a in enumerate(args):
                if _is_ap(a):
                    (outs if i == 0 else ins).append(a)
            for k, a in kw.items():
                if _is_ap(a):
                    (outs if k in ("out", "accum_out") else ins).append(a)
            return s.op(e, lambda eng: getattr(eng, name)(*args, **kw), outs, ins)
        return call


class Sched:
    ENG = ("pe", "act", "dve", "pool", "sp")

    def __init__(self, nc, stack, safe_raw=True):
        self.nc = nc
        self.stack = stack
        self.eng = {"pe": nc.tensor, "act": nc.scalar, "dve": nc.vector, "pool": nc.gpsimd, "sp": nc.sync}
        self.sem, self.cnt, self.known, self.clocks = {}, {}, {}, {}
        for e in self.ENG:
            self.sem[e] = stack.enter_context(nc.semaphore("s_" + e))
            self.cnt[e] = 0
            self.known[e] = {}
        self.safe_raw = safe_raw
        self.trk = Tracker()
        self.ps_touch = {}
        self.n_wait = 0
        self.n_ins = 0
        self.pe = _EngProxy(self, "pe")
        self.act = _EngProxy(self, "act")
        self.dve = _EngProxy(self, "dve")
        self.pool = _EngProxy(self, "pool")

    def new_dma_sem(self, name):
        self.sem[name] = self.stack.enter_context(self.nc.semaphore("d_" + name))
        self.cnt[name] = 0
        return name

    def _collect(self, outs, ins):
        raw, oth = {}, {}
        accs = []
        for a in ins:
            if _on_chip(a):
                arena, p0, p1, runs = _ap_rects(a)
                kind = "r"
                if arena.startswith("ps"):
                    self.ps_touch[arena] = self.n_ins
                    p0, p1, runs, kind = (p0 // 32) * 32, ((p1 + 31) // 32) * 32, [(0, 2048)], "w"
                self.trk.touch(arena, p0, p1, runs, kind, raw)
                accs.append((arena, p0, p1, runs, kind))
        for a in outs:
            if _on_chip(a):
                arena, p0, p1, runs = _ap_rects(a)
                if arena.startswith("ps"):
                    self.ps_touch[arena] = self.n_ins
                    p0, p1, runs = (p0 // 32) * 32, ((p1 + 31) // 32) * 32, [(0, 2048)]
                self.trk.touch(arena, p0, p1, runs, "w", oth)
                accs.append((arena, p0, p1, runs, "w"))
        return raw, oth, accs

    def _do_waits(self, e, raw, oth, is_dma=False):
        known = self.known[e]
        eng = self.eng[e]
        deps = dict(oth)
        for k, v in raw.items():
            if deps.get(k, 0) < v:
                deps[k] = v
        for k, v in deps.items():
            if k == e and not is_dma:
                if e == "pe" or not self.safe_raw:
                    continue
                if not STRICT_SAME_ENGINE:
                    v = raw.get(k, 0)
                    if v == 0:
                        continue
            if known.get(k, 0) >= v:
                continue
            eng.wait_ge(self.sem[k], v)
            self.n_wait += 1
            known[k] = v
            clk = self.clocks.get((k, v))
            if clk:
                for kk, vv in clk.items():
                    if known.get(kk, 0) < vv:
                        known[kk] = vv

    def op(self, e, fn, outs, ins):
        raw, oth, accs = self._collect(outs, ins)
        self._do_waits(e, raw, oth)
        ins_ = fn(self.eng[e])
        self.cnt[e] += 1
        v = self.cnt[e]
        ins_.then_inc(self.sem[e], 1)
        self.n_ins += 1
        self.clocks[(e, v)] = dict(self.known[e])
        for (arena, p0, p1, runs, kind) in accs:
            self.trk.add(arena, p0, p1, runs, kind, (e, v))
        return ins_

    def dma(self, e, out, in_):
        outs = [out] if _on_chip(out) else []
        ins = [in_] if _on_chip(in_) else []
        chip = out if _on_chip(out) else in_
        arena, p0, p1, runs = _ap_rects(chip)
        key = ("dma", e, arena, p0, runs[0][0], "w" if _on_chip(out) else "r")
        if key not in self.sem:
            self.sem[key] = self.stack.enter_context(self.nc.semaphore("d%d" % len(self.sem)))
            self.cnt[key] = 0
        raw, oth, accs = self._collect(outs, ins)
        self._do_waits(e, raw, oth, is_dma=True)
        ins_ = self.eng[e].dma_start(out=out, in_=in_)
        self.cnt[key] += 16
        v = self.cnt[key]
        ins_.then_inc(self.sem[key], 16)
        self.clocks[(key, v)] = dict(self.known[e])
        for (arena, p0, p1, runs, kind) in accs:
            self.trk.add(arena, p0, p1, runs, kind, (key, v))
        self.last_dma = (key, v)
        return ins_


W_IN_BASE = dict(rq=0, rk=256, rv=512, rg=768, pv=1024, nq=1280, nk=1536, nv=1792, mq=2048)
_SWAP = np.array([(p // 64) * 64 + ((p % 64) + 32) % 64 for p in range(128)])
_AR = np.arange(128)


def _cols(name, hp, swap=False):
    return W_IN_BASE[name] + hp * 128 + (_SWAP if swap else _AR)


def layer_plan():
    plan = []
    allr = (0, 1024)

    def add(key, parts):
        n = sum((p[3] // 128) * len(p[4]) for p in parts) if False else None
        kr = parts[0][3] // 128
        n = kr * sum(len(p[4]) for p in parts)
        assert n <= SLOT_ELEMS, (key, n)
        plan.append((key, n, parts))

    for hp in range(2):
        add(("rqk", hp), [("w_in", None, 0, 1024, np.concatenate([_cols("rq", hp), _cols("rk", hp)]))])
        add(("rgv", hp), [("w_in", None, 0, 1024, np.concatenate([_cols("rg", hp), _cols("rv", hp)]))])
    add("pv", [("w_in", None, 0, 1024, np.arange(1024, 1280))])
    add("memk", [("w_mem_kv", None, 0, 1024, np.arange(0, 256))])
    add("memv", [("w_mem_kv", None, 0, 1024, np.arange(256, 512))])
    for hp in range(2):
        add(("nqk", hp), [("w_in", None, 0, 1024, np.concatenate([_cols("nq", hp), _cols("nk", hp)]))])
        add(("nv", hp), [("w_in", None, 0, 1024, _cols("nv", hp))])
    add("mq", [("w_in", None, 0, 1024, np.arange(2048, 2304))])
    for dc in range(8):
        for np_ in range(2):
            gcols = np.concatenate([(2 * np_ + nn) * 1024 + dc * 128 + _AR for nn in range(2)])
            add(("gate", dc, np_), [("w_gate", None, 0, 1024, gcols)])
            add(("br", dc, np_), [("w_branch", 2 * np_ + nn, 0, 256, dc * 128 + _AR) for nn in range(2)])
    for j in range(4):
        add(("wout", j), [("w_out", None, 0, 1024, np.arange(j * 256, (j + 1) * 256))])
    for half in range(2):
        for fi in range(FHALF):
            f = half * FHALF + fi
            add(("ffn1", half, fi), [("w_ffn_in", None, 0, 1024, np.concatenate([f * 128 + _AR, FF + f * 128 + _AR]))])
        for dc in range(8):
            add(("ffn2", half, dc), [("w_ffn_out", None, half * FHALF * 128, FHALF * 128, dc * 128 + _AR)])
    return plan


def pack_slab(weights, l, parts):
    blocks = []
    for (name, sub, r0, nr, cols) in parts:
        w = weights[name][l]
        if sub is not None:
            w = w[sub]
        blk = w[r0:r0 + nr][:, cols]
        blk = blk.reshape(nr // 128, 128, len(cols)).transpose(1, 0, 2)
        blocks.append(blk)
    if len(blocks) == 1:
        out = blocks[0]
    else:
        out = np.stack(blocks, axis=2)
    return np.ascontiguousarray(out.reshape(128, -1), dtype=np.float32)


C_P, C_Q, C_I1, C_I2, C_J1, C_J2, C_VEC = 0, 128, 256, 384, 512, 513, 514
VEC_W = 32
C_DEC = C_VEC + VEC_W * NL
NCST = C_DEC + 8 * NL
B_ID, B_ONE, B_BLK, B_CM, B_BAND = 0, 128, 256, 384, 448
B_PW = B_BAND + 20 * 128
B_PERM = B_PW + 128 * NL
B_H0 = B_PERM + 128
B_H1 = B_H0 + 128
NCBF = B_H1 + 128


def _pool_bands():
    S = S_LEN
    out = np.zeros((4, 5, 128, 128), np.float32)
    t = np.arange(S)
    for g, Wd in enumerate((2, 4, 8, 16)):
        half = Wd // 2
        lo = np.clip(t - half, 0, S)
        hi = np.clip(t + half, 0, S)

        def block(j, jm):
            B = np.zeros((128, 128), np.float64)
            for ti in range(128):
                tg = j * 128 + ti
                for m in range(max(lo[tg], jm * 128), min(hi[tg], (jm + 1) * 128)):
                    B[ti, m - jm * 128] += 1.0 / (hi[tg] - lo[tg])
                if jm == j:
                    B[ti, ti] -= 1.0
            return B.T.astype(np.float32)
        out[g, 0] = block(5, 4)
        out[g, 1] = block(5, 6)
        out[g, 2] = block(0, 0)
        out[g, 3] = block(5, 5)
        out[g, 4] = block(15, 15)
    return out


def _host_consts(inp):
    cst = np.zeros((128, NCST), np.float32)
    m = np.arange(128)[:, None]
    n = np.arange(128)[None, :]
    cst[:, C_P:C_P + 128] = np.maximum(n - m, 0)
    cst[:, C_Q:C_Q + 128] = np.maximum(m - n, 0)
    cst[:, C_I1:C_I1 + 128] = n + 1
    cst[:, C_I2:C_I2 + 128] = 128 - n
    cst[:, C_J1] = 127 - np.arange(128)
    cst[:, C_J2] = np.arange(128)
    p = np.arange(128)
    for l in range(NL):
        o = C_VEC + VEC_W * l
        cst[:, o:o + 8] = inp["norm_mix_g"][l].reshape(8, 128).T
        cst[:, o + 8:o + 16] = inp["norm_mem_g"][l].reshape(8, 128).T
        cst[:, o + 16:o + 24] = inp["norm_ffn_g"][l].reshape(8, 128).T
        cst[:, o + 24:o + 26] = inp["ret_norm_g"][l].reshape(2, 128).T
        cst[:, o + 26:o + 28] = inp["pool_scale"][l].reshape(2, 128).T
        cst[:, o + 28] = inp["na_q_norm_g"][l][p % 64]
        cst[:, o + 29] = inp["na_k_norm_g"][l][p % 64]
        cst[:, o + 30] = inp["mem_q_norm_g"][l][p % 64]
        cst[:, o + 31] = inp["mem_k_norm_g"][l][p % 64]
        cst[:, C_DEC + 8 * l:C_DEC + 8 * l + 4] = inp["ret_decay_fwd"][l][None, :]
        cst[:, C_DEC + 8 * l + 4:C_DEC + 8 * l + 8] = inp["ret_decay_bwd"][l][None, :]
    cbf = np.zeros((128, NCBF), np.float32)
    cbf[:, B_ID:B_ID + 128] = np.eye(128)
    cbf[:, B_ONE:B_ONE + 128] = 1.0
    blk = np.zeros((128, 128), np.float32)
    blk[:64, :64] = 1.0
    blk[64:, 64:] = 1.0
    cbf[:, B_BLK:B_BLK + 128] = blk
    kc = (p % 64)[:, None]
    q = np.arange(64)[None, :]
    qwin = np.clip(q - 8, 0, 48)
    cbf[:, B_CM:B_CM + 64] = np.where((kc >= qwin) & (kc < qwin + 16), 0.0, -1e30)
    bands = _pool_bands()
    cbf[:, B_BAND:B_BAND + 20 * 128] = bands.reshape(20, 128, 128).transpose(1, 0, 2).reshape(128, -1)
    for l in range(NL):
        pw = inp["pool_w"][l]
        blkw = np.zeros((128, 2, 64), np.float32)
        for g in range(4):
            blkw[(g % 2) * 64:(g % 2) * 64 + 64, g // 2, :] = pw[g]
        cbf[:, B_PW + 128 * l:B_PW + 128 * (l + 1)] = blkw.reshape(128, 128)
    cbf[_SWAP, B_PERM + np.arange(128)] = 1.0
    cbf[:, B_H0:B_H0 + 64] = 1.0
    cbf[:, B_H1 + 64:B_H1 + 128] = 1.0
    half = 32
    inv = (10000.0 ** (-np.arange(half, dtype=np.float32) / half)).astype(np.float32)
    ang = (np.arange(S_LEN, dtype=np.float32)[None, :] * inv[:, None]).astype(np.float32)
    d = p % 64
    rope = np.zeros((2, 128, S_LEN), np.float32)
    rope[0] = np.cos(ang)[d % 32]
    sgn = np.where(d < 32, -1.0, 1.0).astype(np.float32)[:, None]
    rope[1] = np.sin(ang)[d % 32] * sgn
    mst = np.zeros((NL, 128, 2, 2, 8, 2, 64), np.float32)
    kcol = (p % 64)[:, None]
    qq = np.arange(64)[None, :]
    coff = np.clip(kcol - qq, -15, 15) + 15
    qwin_ = np.clip(qq - 8, 0, 48)
    cmask_ok = (kcol >= qwin_) & (kcol < qwin_ + 16)
    for par in range(2):
        for s in range(8):
            roff = 2 * s + (p // 64) - (8 if par == 0 else 7)
            ok = (roff >= -7) & (roff <= 7)
            ridx = np.clip(roff, -7, 7) + 7
            for l in range(NL):
                for h in range(4):
                    vals = inp["na_rpb"][l][h][ridx[:, None], coff]
                    mst[l, :, par, h // 2, s, h % 2, :] = np.where(ok[:, None] & cmask_ok, vals, -1e30)
    return cst, cbf, rope, mst.reshape(NL, 128, -1)


class Arena:
    def __init__(self, big, base, size):
        self.big, self.base, self.size, self.top = big, base, size, 0

    def alloc(self, shape, dtype):
        n = int(np.prod(shape))
        nb = n * mybir.dt.size(dtype)
        nb_al = (nb + 63) // 64 * 64
        assert self.top + nb_al <= self.size, ("arena overflow", self.top, nb_al, self.size)
        off = self.base + self.top
        self.top += nb_al
        v = self.big[:, off // 4:(off + nb_al) // 4]
        if dtype != F32:
            v = v.bitcast(dtype)
        v = v[:, 0:n]
        if len(shape) == 2:
            v = v.rearrange("p (a b) -> p a b", a=shape[0])
        elif len(shape) == 3:
            v = v.rearrange("p (a b c) -> p a b c", a=shape[0], b=shape[1])
        elif len(shape) == 4:
            v = v.rearrange("p (a b c d) -> p a b c d", a=shape[0], b=shape[1], c=shape[2])
        elif len(shape) == 5:
            v = v.rearrange("p (a b c d e) -> p a b c d e", a=shape[0], b=shape[1], c=shape[2], d=shape[3])
        return v

    def mark(self):
        return self.top

    def release(self, m):
        self.top = m


class Rot:
    def __init__(self, arena, n, shape, dtype):
        self.bufs = [arena.alloc(shape, dtype) for _ in range(n)]
        self.i = 0

    def next(self):
        b = self.bufs[self.i % len(self.bufs)]
        self.i += 1
        return b


class _Stop(Exception):
    pass


def build_program(nl, taps=(), stop_after=None):
    nc = bass.Bass("TRN2", target_bir_lowering=False)
    plan1 = layer_plan()
    n_per_layer = sum(128 * n for _, n, _ in plan1)
    xT_d = nc.dram_tensor("xT", [D, S_LEN], F32, kind="ExternalInput").ap()
    memT_d = nc.dram_tensor("memT", [D, 256], F32, kind="ExternalInput").ap()
    cst_d = nc.dram_tensor("cst", [128, NCST], F32, kind="ExternalInput").ap()
    cbf_d = nc.dram_tensor("cbf", [128, NCBF], F32, kind="ExternalInput").ap()
    rope_d = nc.dram_tensor("rope", [2, 128, S_LEN], F32, kind="ExternalInput").ap()
    mst_d = nc.dram_tensor("namst", [NL, 128, 4096], F32, kind="ExternalInput").ap()
    wts_d = nc.dram_tensor("wts", [nl * n_per_layer], F32, kind="ExternalInput").ap()
    outT_d = nc.dram_tensor("outT", [D, S_LEN], F32, kind="ExternalOutput").ap()
    tap_d = {}
    for name, shape, dt in taps:
        tap_d[name] = nc.dram_tensor("tap_" + name, [128] + list(shape), dt, kind="ExternalOutput").ap()

    TOTAL = 209920
    big = nc.alloc_sbuf_tensor("big", [128, TOTAL // 4], F32)
    psb = [nc.alloc_psum_tensor(f"ps{i}", [128, 512], F32) for i in range(8)]
    OFF_X = 0
    OFF_H = 65536
    OFF_RING = OFF_H + 32768
    OFF_CST = OFF_RING + NSLOT * SLOT_ELEMS * 2
    CST_BYTES = 11776
    OFF_WORK = OFF_CST + CST_BYTES
    WORK_BYTES = TOTAL - OFF_WORK
    xT = big[:, OFF_X // 4:(OFF_X + 65536) // 4].rearrange("p (c t) -> p c t", c=KC)
    hT = big[:, OFF_H // 4:(OFF_H + 32768) // 4].bitcast(BF16).rearrange("p (c t) -> p c t", c=KC)
    ring = [big[:, (OFF_RING + i * 4096) // 4:(OFF_RING + (i + 1) * 4096) // 4].bitcast(BF16) for i in range(NSLOT)]
    carena = Arena(big, OFF_CST, CST_BYTES)
    work = Arena(big, OFF_WORK, WORK_BYTES)
    cst = carena.alloc([NCST], F32)
    cbf = carena.alloc([NCBF], BF16)
    rstd_mem = carena.alloc([256], F32)
    lgall = carena.alloc([8 * NL], F32)
    ident = cbf[:, B_ID:B_ID + 128]
    ones = cbf[:, B_ONE:B_ONE + 128]
    blkones = cbf[:, B_BLK:B_BLK + 128]

    def tsl(tt):
        return slice(tt * TT, (tt + 1) * TT)

    with contextlib.ExitStack() as stack:
        S = Sched(nc, stack, safe_raw=SAFE_RAW)
        pe, act, dve = S.pe, S.act, S.dve

        psi = [0]
        held = set()

        def PS(hold=False):
            cand = [i for i in range(8) if i not in held]
            i = min(cand, key=lambda j: (S.ps_touch.get("ps%d" % j, -1), (j - psi[0]) % 8))
            psi[0] = i + 1
            S.ps_touch["ps%d" % i] = S.n_ins
            if hold:
                held.add(i)
            return psb[i]

        def unhold(*pss):
            for p_ in pss:
                held.discard(psb.index(p_))

        def tap(name, ap):
            if name in tap_d:
                S.dma("sp", tap_d[name], ap)

        full_plan = [(l, k, n, parts) for l in range(nl) for (k, n, parts) in plan1]
        offs = np.cumsum([0] + [128 * n for (_, _, n, _) in full_plan])
        ws = dict(issue=0, use=0)

        def w_issue(j):
            l, k, n, parts = full_plan[j]
            slot = j % NSLOT
            src = wts_d[int(offs[j]):int(offs[j]) + 128 * n].rearrange("(p l) -> p l", p=128)
            S.dma("pool", ring[slot][:, 0:n], src)

        def w_acquire(l, keys):
            first = ws["use"]
            views = []
            for i, k in enumerate(keys):
                fl, fk, fn_, _ = full_plan[first + i]
                assert fl == l and fk == k, (fl, fk, l, k)
                views.append(ring[(first + i) % NSLOT])
            ws["use"] += len(keys)
            limit = min(first + NSLOT - 1, len(full_plan) - 1)
            while ws["issue"] <= limit:
                w_issue(ws["issue"])
                ws["issue"] += 1
            return views

        def k8(view, ncols):
            return view[:, 0:8 * ncols].rearrange("p (k c) -> p k c", k=8)

        S.dma("pool", cbf, cbf_d)
        S.dma("sp", cst, cst_d)
        for tt in range(NT):
            for c in range(KC):
                S.dma("sp", xT[:, c, tsl(tt)], xT_d[c * 128:(c + 1) * 128, tsl(tt)])

        dec = cst[:, C_DEC:C_DEC + 8 * nl]
        act.activation(out=lgall[:, 0:8 * nl], in_=dec, func=AF.Exp, scale=-1.0)
        act.activation(out=lgall[:, 0:8 * nl], in_=lgall[:, 0:8 * nl], func=AF.Ln, bias=1.0)
        act.mul(out=lgall[:, 0:8 * nl], in_=lgall[:, 0:8 * nl], mul=-1.0)

        def norm_make(gcol0, sqr, rsr):
            st = {}

            def s1(tt):
                sq = sqr.next()
                act.activation(out=sq, in_=xT[:, :, tsl(tt)], func=AF.Square)
                ps = PS(hold=True)
                for c in range(KC):
                    pe.matmul(ps[:, :], lhsT=ones, rhs=sq[:, c, :], start=(c == 0), stop=(c == KC - 1))
                st[tt] = ps

            def s2(tt):
                ps = st.pop(tt)
                rs = rsr.next()
                act.activation(out=rs, in_=ps[:, :], func=AF.Ln, scale=1.0 / D, bias=EPS)
                unhold(ps)
                act.activation(out=rs, in_=rs, func=AF.Exp, scale=-0.5)
                for c in range(KC):
                    dve.scalar_tensor_tensor(out=hT[:, c, tsl(tt)], in0=xT[:, c, tsl(tt)],
                                             scalar=cst[:, gcol0 + c:gcol0 + c + 1], in1=rs,
                                             op0=ALU.mult, op1=ALU.mult)
            return s1, s2

        def proj_fm(wv, col0, tt, src=None):
            src = hT if src is None else src
            ps = PS()
            for c in range(KC):
                pe.matmul(ps[:, :], lhsT=wv[:, c, col0:col0 + 128], rhs=src[:, c, tsl(tt)],
                          start=(c == 0), stop=(c == KC - 1))
            return ps

        def qk_run(items, sqr, rsr, n=TT):
            st = {}

            def s1(i):
                ps = items[i][0]()
                sq = sqr.next()
                act.activation(out=sq[:, 0:n], in_=ps[:, 0:n], func=AF.Square)
                st[i] = (ps, sq)

            def s2(i):
                ps, sq = st.pop(i)
                _, gcol, outs = items[i]
                pn = PS()
                pe.matmul(pn[:, 0:n], lhsT=blkones, rhs=sq[:, 0:n], start=True, stop=True)
                rs = rsr.next()
                act.activation(out=rs[:, 0:n], in_=pn[:, 0:n], func=AF.Ln, scale=1.0 / 64, bias=EPS)
                act.activation(out=rs[:, 0:n], in_=rs[:, 0:n], func=AF.Exp, scale=-0.5)
                for rows, out_ap in outs:
                    i0, i1 = ps[rows, 0:n], rs[rows, 0:n]
                    if len(out_ap.shape) == 3:
                        i0 = i0.rearrange("p (a b) -> p a b", a=out_ap.shape[1])
                        i1 = i1.rearrange("p (a b) -> p a b", a=out_ap.shape[1])
                    gsc = cst[rows, gcol:gcol + 1] if isinstance(gcol, int) else gcol[rows, 0:1]
                    dve.scalar_tensor_tensor(out=out_ap, in0=i0, scalar=gsc,
                                             in1=i1, op0=ALU.mult, op1=ALU.mult)
            s1(0)
            for i in range(len(items)):
                if i + 1 < len(items):
                    s1(i + 1)
                s2(i)

        ALLR = slice(0, 128)
        R0 = slice(0, 64)
        R1 = slice(64, 128)
        perm = cbf[:, B_PERM:B_PERM + 128]
        hones = [cbf[:, B_H0:B_H0 + 128], cbf[:, B_H1:B_H1 + 128]]

        try:
            for l in range(nl):
                vo = C_VEC + VEC_W * l
                wtop = work.mark()
                m_n = work.mark()
                n_sqr = Rot(work, 2, [KC, TT], BF16)
                n_rsr = Rot(work, 2, [TT], F32)
                work.release(m_n)
                n_s1, n_s2 = norm_make(vo + 0, n_sqr, n_rsr)
                n_s1(0)
                n_s1(1)
                n_s2(0)

                def norm_hook(i):
                    if i + 2 < NT:
                        n_s1(i + 2)
                    n_s2(i + 1)
                br = [work.alloc([2, S_LEN], BF16) for _ in range(4)]
                retT, poolT, naT, moT = br
                ptop = work.mark()

                for hp in range(2):
                    m_hp = work.mark()
                    krot = work.alloc([S_LEN], BF16)
                    qblk = work.alloc([16, 2, 128], BF16)
                    DT = work.alloc([2, 128], F32)
                    lgc = work.alloc([4], F32)
                    qfall = work.alloc([2, S_LEN], BF16)
                    cf = 8 * l + 2 * hp
                    cb = 8 * l + 4 + 2 * hp
                    S.pool.memset(qblk[R0, :, 1, :], 0.0)
                    S.pool.memset(qblk[R1, :, 0, :], 0.0)

                    m_rot = work.mark()
                    QD = work.alloc([2, 128], F32)
                    csr = Rot(work, 3, [2, TT], F32)
                    tar = Rot(work, 2, [TT], F32)
                    tbr = Rot(work, 2, [TT], F32)
                    qbr = Rot(work, 2, [TT], BF16)
                    qrot = work.alloc([S_LEN], BF16)
                    (wq,) = w_acquire(l, [("rqk", hp)])
                    wqv = k8(wq, 256)
                    rot_st = {}

                    def rot_1(i):
                        col0, tt = (0, i) if i < NT else (128, i - NT)
                        cs_ = csr.next()
                        S.dma("sp", cs_[:, 0, :], rope_d[0, :, tsl(tt)])
                        S.dma("sp", cs_[:, 1, :], rope_d[1, :, tsl(tt)])
                        pa = proj_fm(wqv, col0, tt)
                        qb = qbr.next()
                        act.copy(out=qb, in_=pa[:, :])
                        rot_st[i] = (cs_, pa, qb)

                    def rot_2(i):
                        dst, tt = (qrot, i) if i < NT else (krot, i - NT)
                        cs_, pa, qb = rot_st.pop(i)
                        pb = PS()
                        pe.matmul(pb[:, :], lhsT=perm, rhs=qb, start=True, stop=True)
                        ta = tar.next()
                        tb = tbr.next()
                        dve.tensor_tensor(out=ta, in0=pa[:, :], in1=cs_[:, 0, :], op=ALU.mult)
                        dve.tensor_tensor(out=tb, in0=pb[:, :], in1=cs_[:, 1, :], op=ALU.mult)
                        dve.tensor_tensor(out=dst[:, tsl(tt)], in0=ta, in1=tb, op=ALU.add)
                        if i < NT:
                            for hh, rows in ((0, R0), (1, R1)):
                                act.copy(out=qblk[rows, 4 * tt:4 * tt + 4, hh, :],
                                         in_=qrot[rows, tsl(tt)].rearrange("p (a b) -> p a b", a=4))
                    rot_1(0)
                    if hp == 0:
                        norm_hook(0)
                    for i in range(2 * NT):
                        if i + 1 < 2 * NT:
                            rot_1(i + 1)
                            if hp == 0 and i + 1 < NT - 1:
                                norm_hook(i + 1)
                        rot_2(i)
                    if hp == 0 and l == 0:
                        tap("h", hT[:, :, :])
                    for hh in range(2):
                        rows = slice(hh * 64, (hh + 1) * 64)
                        dve.tensor_copy(out=lgc[rows, 0:1], in_=lgall[rows, cf + hh:cf + hh + 1])
                        dve.tensor_copy(out=lgc[rows, 1:2], in_=lgall[rows, cb + hh:cb + hh + 1])
                    act.activation(out=lgc[:, 2:4], in_=lgc[:, 0:2], func=AF.Exp, scale=128.0)
                    for hh in range(2):
                        dve.tensor_scalar(out=DT[:, hh, :], in0=cst[:, C_P:C_P + 128],
                                          scalar1=lgall[:, cf + hh:cf + hh + 1], scalar2=None, op0=ALU.mult)
                        dve.scalar_tensor_tensor(out=DT[:, hh, :], in0=cst[:, C_Q:C_Q + 128],
                                                 scalar=lgall[:, cb + hh:cb + hh + 1], in1=DT[:, hh, :],
                                                 op0=ALU.mult, op1=ALU.add)
                    act.activation(out=DT, in_=DT, func=AF.Exp, bias=LN8)
                    act.activation(out=QD[:, 0, :], in_=cst[:, C_I1:C_I1 + 128], func=AF.Exp, scale=lgc[:, 0:1], bias=LN8)
                    act.activation(out=QD[:, 1, :], in_=cst[:, C_I2:C_I2 + 128], func=AF.Exp, scale=lgc[:, 1:2], bias=LN8)
                    for tt in range(NT):
                        for fb in range(2):
                            S.pool.tensor_tensor(out=qfall[:, fb, tsl(tt)].rearrange("p (a b) -> p a b", a=4),
                                                 in0=qrot[:, tsl(tt)].rearrange("p (a b) -> p a b", a=4),
                                                 in1=QD[:, fb, :].unsqueeze(1).to_broadcast([128, 4, 128]),
                                                 op=ALU.mult)
                    work.release(m_rot)

                    silu_rg = work.alloc([S_LEN], BF16)
                    vtok = work.alloc([16, 128], BF16)
                    m_sb = work.mark()
                    kdr = Rot(work, 2, [2, 4, 128], BF16)
                    work.release(m_sb)
                    st_blk = work.alloc([16, 2, 128], BF16)
                    (wg,) = w_acquire(l, [("rgv", hp)])
                    wgv = k8(wg, 256)
                    for g4 in range(4):
                        ps = PS()
                        for i in range(4):
                            j = g4 * 4 + i
                            for c in range(KC):
                                pe.matmul(ps[:, i * 128:(i + 1) * 128], lhsT=hT[:, c, j * 128:(j + 1) * 128],
                                          rhs=wgv[:, c, 128:256], start=(c == 0), stop=(c == KC - 1))
                        act.copy(out=vtok[:, g4 * 4:(g4 + 1) * 4, :], in_=ps[:, :].rearrange("p (a b) -> p a b", a=4))

                    m_kv = work.mark()
                    KD = work.alloc([2, 128], F32)
                    act.activation(out=KD[:, 0, :].rearrange("p (h d) -> p h d", h=2),
                                   in_=lgall[:, cf:cf + 2].unsqueeze(2).to_broadcast([128, 2, 64]),
                                   func=AF.Exp, scale=cst[:, C_J1:C_J1 + 1])
                    act.activation(out=KD[:, 1, :].rearrange("p (h d) -> p h d", h=2),
                                   in_=lgall[:, cb:cb + 2].unsqueeze(2).to_broadcast([128, 2, 64]),
                                   func=AF.Exp, scale=cst[:, C_J2:C_J2 + 1])
                    kvs = work.alloc([16, 2, 64], F32)

                    def kv_A(g4):
                        pt = PS()
                        ptb = pt[:, 0:256].bitcast(BF16).rearrange("p (a b) -> p a b", a=4)
                        for i in range(4):
                            c_ = g4 * 4 + i
                            pe.transpose(out=ptb[:, i, :], in_=krot[:, c_ * 128:(c_ + 1) * 128], identity=ident)
                        kd = kdr.next()
                        for fb in range(2):
                            dve.tensor_tensor(out=kd[:, fb, :, :], in0=ptb,
                                              in1=KD[:, fb, :].unsqueeze(1).to_broadcast([128, 4, 128]), op=ALU.mult)
                        return kd

                    def kv_B(g4, kd):
                        for half in range(2):
                            pk = PS()
                            pkv = pk[:, :].rearrange("p (i f e) -> p i f e", i=2, f=2)
                            for ii in range(2):
                                i = half * 2 + ii
                                c_ = g4 * 4 + i
                                for fb in range(2):
                                    pe.matmul(pkv[:, ii, fb, :], lhsT=kd[:, fb, i, :], rhs=vtok[:, c_, :],
                                              start=True, stop=True)
                            c0 = g4 * 4 + half * 2
                            act.copy(out=kvs[R0, c0:c0 + 2, :, :], in_=pkv[R0, :, :, 0:64])
                            act.copy(out=kvs[R1, c0:c0 + 2, :, :], in_=pkv[R1, :, :, 64:128])
                    kds = {0: kv_A(0)}
                    for g4 in range(4):
                        if g4 + 1 < 4:
                            kds[g4 + 1] = kv_A(g4 + 1)
                        kv_B(g4, kds[g4])
                    for tt in range(NT):
                        pg = proj_fm(wgv, 0, tt)
                        act.activation(out=silu_rg[:, tsl(tt)], in_=pg[:, :], func=AF.Silu)
                    for j_ in range(1, 16):
                        c_ = j_
                        dve.scalar_tensor_tensor(out=kvs[:, c_, 0, :], in0=kvs[:, c_ - 1, 0, :], scalar=lgc[:, 2:3],
                                                 in1=kvs[:, c_, 0, :], op0=ALU.mult, op1=ALU.add)
                        c_ = 15 - j_
                        dve.scalar_tensor_tensor(out=kvs[:, c_, 1, :], in0=kvs[:, c_ + 1, 1, :], scalar=lgc[:, 3:4],
                                                 in1=kvs[:, c_, 1, :], op0=ALU.mult, op1=ALU.add)
                    S.pool.memset(st_blk[R0, :, :, 64:128], 0.0)
                    S.pool.memset(st_blk[R1, :, :, 0:64], 0.0)
                    act.copy(out=st_blk[R0, :, :, 0:64], in_=kvs[R0, :, :, :])
                    dve.tensor_copy(out=st_blk[R1, :, :, 64:128], in_=kvs[R1, :, :, :])
                    work.release(m_kv)

                    m_o = work.mark()
                    ptr_ = Rot(work, 2, [4, 2, 128], BF16)
                    sqr = Rot(work, 2, [TT], BF16)
                    rsr = Rot(work, 2, [TT], F32)
                    tmr = Rot(work, 1, [TT], F32)
                    pTs, pos_ = {}, {}

                    def out_A(g4):
                        pT = ptr_.next()
                        for half in range(2):
                            pss = PS()
                            for ii in range(2):
                                c_ = g4 * 4 + half * 2 + ii
                                pe.matmul(pss[:, ii * 256:(ii + 1) * 256], lhsT=krot[:, c_ * 128:(c_ + 1) * 128],
                                          rhs=qblk[:, c_, :, :].rearrange("p g n -> p (g n)"), start=True, stop=True)
                            dve.tensor_tensor(out=pT[:, half * 2:half * 2 + 2, :, :],
                                              in0=pss[:, :].rearrange("p (c g n) -> p c g n", c=2, g=2),
                                              in1=DT[:, :, :].unsqueeze(1).to_broadcast([128, 2, 2, 128]), op=ALU.mult)
                        pTs[g4] = pT

                    def out_B(g4):
                        pT = pTs.pop(g4)
                        regs = [PS(hold=True), PS(hold=True)]
                        for i in range(4):
                            c_ = g4 * 4 + i
                            has_f = c_ > 0
                            has_b = c_ < 15
                            tokc = slice(c_ * 128, (c_ + 1) * 128)
                            for hh in range(2):
                                o_ap = regs[hh][:, i * 128:(i + 1) * 128]
                                pe.matmul(o_ap, lhsT=vtok[:, c_, :], rhs=pT[:, i, hh, :], start=True,
                                          stop=not (has_f or has_b))
                                if has_f:
                                    pe.matmul(o_ap, lhsT=st_blk[:, c_ - 1, 0, :], rhs=qfall[:, 0, tokc],
                                              start=False, stop=not has_b)
                                if has_b:
                                    pe.matmul(o_ap, lhsT=st_blk[:, c_ + 1, 1, :], rhs=qfall[:, 1, tokc],
                                              start=False, stop=True)
                        pos_[g4] = regs

                    def out_C(g4):
                        tok4 = slice(g4 * 512, (g4 + 1) * 512)
                        po = pos_.pop(g4)
                        sq = sqr.next()
                        for hh, rows in ((0, R0), (1, R1)):
                            act.activation(out=sq[rows, :], in_=po[hh][rows, :], func=AF.Square)
                        pn = PS()
                        pe.matmul(pn[:, :], lhsT=blkones, rhs=sq, start=True, stop=True)
                        rs = rsr.next()
                        act.activation(out=rs, in_=pn[:, :], func=AF.Ln, scale=1.0 / 64, bias=EPS)
                        act.activation(out=rs, in_=rs, func=AF.Exp, scale=-0.5)
                        tm = tmr.next()
                        for hh, rows in ((0, R0), (1, R1)):
                            dve.scalar_tensor_tensor(out=tm[rows, :], in0=po[hh][rows, :],
                                                     scalar=cst[rows, vo + 24 + hp:vo + 25 + hp], in1=rs[rows, :],
                                                     op0=ALU.mult, op1=ALU.mult)
                        dve.tensor_tensor(out=retT[:, hp, tok4], in0=tm, in1=silu_rg[:, tok4], op=ALU.mult)
                        unhold(*po)

                    out_A(0)
                    for g4 in range(4):
                        if g4 + 1 < 4:
                            out_A(g4 + 1)
                        out_B(g4)
                        if g4 >= 1:
                            out_C(g4 - 1)
                    out_C(3)
                    work.release(m_o)
                    work.release(m_hp)
                if l == 0:
                    tap("ret", retT)
                if stop_after == "ret":
                    raise _Stop()

                mkT = work.alloc([2, 256], BF16)
                mvm = work.alloc([2, 4, 128], BF16)
                mtop = work.mark()
                memx = work.alloc([KC, 256], F32)
                for c in range(KC):
                    S.dma("sp", memx[:, c, :], memT_d[c * 128:(c + 1) * 128, :])
                if l == 0:
                    msq = work.alloc([KC, 256], BF16)
                    act.activation(out=msq, in_=memx, func=AF.Square)
                    ps = PS()
                    for c in range(KC):
                        pe.matmul(ps[:, 0:256], lhsT=ones, rhs=msq[:, c, :], start=(c == 0), stop=(c == KC - 1))
                    act.activation(out=rstd_mem, in_=ps[:, 0:256], func=AF.Ln, scale=1.0 / D, bias=EPS)
                    act.activation(out=rstd_mem, in_=rstd_mem, func=AF.Exp, scale=-0.5)
                memn = work.alloc([KC, 256], BF16)
                for c in range(KC):
                    dve.scalar_tensor_tensor(out=memn[:, c, :], in0=memx[:, c, :], scalar=cst[:, vo + 8 + c:vo + 9 + c],
                                             in1=rstd_mem, op0=ALU.mult, op1=ALU.mult)
                m_p = work.mark()
                pvt = work.alloc([16, 256], BF16)
                (wp,) = w_acquire(l, ["pv"])
                wpv = k8(wp, 256)
                for g2 in range(8):
                    ps = PS()
                    for i in range(2):
                        j = g2 * 2 + i
                        for c in range(KC):
                            pe.matmul(ps[:, i * 256:(i + 1) * 256], lhsT=hT[:, c, j * 128:(j + 1) * 128],
                                      rhs=wpv[:, c, :], start=(c == 0), stop=(c == KC - 1))
                    act.copy(out=pvt[:, g2 * 2:(g2 + 1) * 2, :], in_=ps[:, :].rearrange("p (a b) -> p a b", a=2))
                pldr = Rot(work, 2, [TT], BF16)
                pw = cbf[:, B_PW + 128 * l:B_PW + 128 * (l + 1)].rearrange("p (a b) -> p a b", a=2)

                def band(g, v):
                    o = B_BAND + (g * 5 + v) * 128
                    return cbf[:, o:o + 128]
                plds = {}

                def pool_A(k):
                    gp, tg = divmod(k, 4)
                    psp = PS()
                    for gg in range(2):
                        g = 2 * gp + gg
                        rows = slice(gg * 64, (gg + 1) * 64)
                        for i in range(4):
                            j = tg * 4 + i
                            dms = [dm for dm in (-1, 0, 1) if 0 <= j + dm <= 15]
                            for k_, dm in enumerate(dms):
                                if dm == -1:
                                    v = 0
                                elif dm == 1:
                                    v = 1
                                else:
                                    v = 2 if j == 0 else (4 if j == 15 else 3)
                                pe.matmul(psp[rows, i * 128:(i + 1) * 128], lhsT=pvt[:, j + dm, g * 64:(g + 1) * 64],
                                          rhs=band(g, v), start=(k_ == 0), stop=(k_ == len(dms) - 1))
                    pld = pldr.next()
                    act.copy(out=pld, in_=psp[:, :])
                    plds[k] = pld

                def pool_B(k):
                    gp, tg = divmod(k, 4)
                    pld = plds.pop(k)
                    for gg in range(2):
                        rows = slice(gg * 64, (gg + 1) * 64)
                        psm = PS()
                        pe.matmul(psm[rows, :], lhsT=pw[rows, gp, :], rhs=pld[rows, :], start=True, stop=True)
                        dve.tensor_scalar(out=poolT[rows, gp, tsl(tg)], in0=psm[rows, :],
                                          scalar1=cst[rows, vo + 26 + gp:vo + 27 + gp], scalar2=None, op0=ALU.mult)
                pool_A(0)
                for k in range(8):
                    if k + 1 < 8:
                        pool_A(k + 1)
                    pool_B(k)
                work.release(m_p)
                sqr = Rot(work, 2, [TT], BF16)
                rsr = Rot(work, 2, [TT], F32)
                (wk,) = w_acquire(l, ["memk"])
                wkv = k8(wk, 256)

                def mk_proj(hp):
                    def f():
                        ps = PS()
                        for c in range(KC):
                            pe.matmul(ps[:, 0:256], lhsT=wkv[:, c, hp * 128:(hp + 1) * 128], rhs=memn[:, c, :],
                                      start=(c == 0), stop=(c == KC - 1))
                        return ps
                    return f
                qk_run([(mk_proj(hp), vo + 31, [(ALLR, mkT[:, hp, :])]) for hp in range(2)], sqr, rsr, n=256)
                (wv_,) = w_acquire(l, ["memv"])
                wvv = k8(wv_, 256)
                S.pool.memset(mvm, 0.0)
                for mt in range(2):
                    ps = PS()
                    for c in range(KC):
                        pe.matmul(ps[:, 0:256], lhsT=memn[:, c, mt * 128:(mt + 1) * 128], rhs=wvv[:, c, :],
                                  start=(c == 0), stop=(c == KC - 1))
                    for hh in range(2):
                        act.copy(out=mvm[:, mt, hh::2, hh * 64:(hh + 1) * 64],
                                 in_=ps[:, 0:256].rearrange("p (a b) -> p a b", a=2)[:, :, hh * 64:(hh + 1) * 64])
                work.release(mtop)
                mem_keep = work.mark()
                if stop_after == "memkv":
                    raise _Stop()

                if l == 0:
                    tap("pool", poolT)
                if stop_after == "pool":
                    raise _Stop()

                m_na = work.mark()
                mst = work.alloc([2, 2, 8, 2, 64], BF16)
                S.dma("pool", mst.rearrange("p a h s g q -> p (a h s g q)"), mst_d[l])
                gq8 = work.alloc([1], F32)
                dve.tensor_scalar(out=gq8, in0=cst[:, vo + 28:vo + 29], scalar1=0.125, scalar2=None, op0=ALU.mult)
                for hp in range(2):
                    m_hp = work.mark()
                    nqb = work.alloc([32, 2, 64], BF16)
                    nkT = work.alloc([S_LEN], BF16)
                    nvt = work.alloc([16, 128], BF16)
                    nvs = work.alloc([15, 128], BF16)
                    sqr = Rot(work, 2, [TT], BF16)
                    rsr = Rot(work, 2, [TT], F32)
                    S.pool.memset(nqb[R0, :, 1, :], 0.0)
                    S.pool.memset(nqb[R1, :, 0, :], 0.0)
                    (wn,) = w_acquire(l, [("nqk", hp)])
                    wnv = k8(wn, 256)
                    items = []
                    for tt in range(NT):
                        items.append(((lambda tt=tt: proj_fm(wnv, 0, tt)), gq8,
                                      [(R0, nqb[R0, tt * 8:(tt + 1) * 8, 0, :]), (R1, nqb[R1, tt * 8:(tt + 1) * 8, 1, :])]))
                    for tt in range(NT):
                        items.append(((lambda tt=tt: proj_fm(wnv, 128, tt)), vo + 29, [(ALLR, nkT[:, tsl(tt)])]))
                    qk_run(items, sqr, rsr)
                    (wnv_,) = w_acquire(l, [("nv", hp)])
                    wvv = k8(wnv_, 128)
                    for g4 in range(4):
                        ps = PS()
                        for i in range(4):
                            j = g4 * 4 + i
                            for c in range(KC):
                                pe.matmul(ps[:, i * 128:(i + 1) * 128], lhsT=hT[:, c, j * 128:(j + 1) * 128],
                                          rhs=wvv[:, c, :], start=(c == 0), stop=(c == KC - 1))
                        act.copy(out=nvt[:, g4 * 4:(g4 + 1) * 4, :],
                                 in_=ps[:, :].rearrange("p (a b) -> p a b", a=4))
                    S.dma("sp", nvs[0:64, :, :], nvt[64:128, 0:15, :])
                    S.dma("sp", nvs[64:128, :, :], nvt[0:64, 1:16, :])
                    ptr_ = Rot(work, 4, [4, 2, 64], BF16)
                    rcr = Rot(work, 2, [256], F32)
                    na_pT, na_acc = {}, {}

                    def na_A(r):
                        r0 = min(max(r - 4, 0), 24)
                        dl = r0 - r
                        par = dl & 1
                        s0 = (dl + 8) // 2 if par == 0 else (dl + 7) // 2
                        pss = PS()
                        pe.matmul(pss[:, :], lhsT=ident,
                                  rhs=mst[:, par, hp, s0:s0 + 4, :, :].rearrange("p s g q -> p (s g q)"),
                                  start=True, stop=False)
                        for i in range(4):
                            k0 = 64 * (r0 + 2 * i)
                            pe.matmul(pss[:, i * 128:(i + 1) * 128], lhsT=nkT[:, k0:k0 + 128],
                                      rhs=nqb[:, r, :, :].rearrange("p g q -> p (g q)"), start=False, stop=(i == 3))
                        pT = ptr_.next()
                        act.activation(out=pT.rearrange("p s g q -> p (s g q)"), in_=pss[:, :], func=AF.Exp)
                        na_pT[r] = pT

                    def na_B(r):
                        r0 = min(max(r - 4, 0), 24)
                        if r % 4 == 0:
                            na_acc[r // 4] = (PS(hold=True), PS(hold=True))
                        pso, psd = na_acc[r // 4]
                        pov = pso[:, :].rearrange("p (g r q) -> p g r q", g=2, r=4)
                        pdv = psd[:, :].rearrange("p (r g q) -> p r g q", r=4, g=2)
                        pT = na_pT.pop(r)
                        for hh in range(2):
                            for i in range(4):
                                if r0 % 2 == 0:
                                    vsrc = nvt[:, (r0 + 2 * i) // 2, :]
                                else:
                                    vsrc = nvs[:, (r0 - 1 + 2 * i) // 2, :]
                                pe.matmul(pov[:, hh, r % 4, :], lhsT=vsrc, rhs=pT[:, i, hh, :], start=(i == 0), stop=(i == 3))
                        for i in range(4):
                            pe.matmul(psd[:, (r % 4) * 128:(r % 4 + 1) * 128], lhsT=ones,
                                      rhs=pT[:, i, :, :].rearrange("p g q -> p (g q)"), start=(i == 0), stop=(i == 3))
                        if r % 4 == 3:
                            t0 = (r // 4) * 256
                            for hh, rows in ((0, R0), (1, R1)):
                                rc = rcr.next()
                                dve.reciprocal(out=rc[rows, :].rearrange("p (r q) -> p r q", r=4), in_=pdv[rows, :, hh, :])
                                dve.tensor_tensor(out=naT[rows, hp, t0:t0 + 256], in0=pov[rows, hh, :, :].rearrange("p r q -> p (r q)"),
                                                  in1=rc[rows, :], op=ALU.mult)
                            unhold(pso, psd)

                    na_A(0)
                    na_A(1)
                    for r in range(32):
                        if r + 2 < 32:
                            na_A(r + 2)
                        na_B(r)
                    work.release(m_hp)
                work.release(m_na)
                if l == 0:
                    tap("na", naT)
                if stop_after == "na":
                    raise _Stop()

                m_m = work.mark()
                (wm,) = w_acquire(l, ["mq"])
                wmv = k8(wm, 256)
                sqr = Rot(work, 2, [TT], BF16)
                rsr = Rot(work, 2, [TT], F32)
                ptr_ = Rot(work, 4, [2, TT], BF16)
                rcr = Rot(work, 2, [TT], F32)
                mqm = [work.alloc([S_LEN], BF16) for _ in range(2)]
                S.pool.memset(mqm[0][R1, :], 0.0)
                S.pool.memset(mqm[1][R0, :], 0.0)
                for hp in range(2):
                    qk_run([((lambda tt=tt: proj_fm(wmv, hp * 128, tt)), vo + 30,
                             [(R0, mqm[0][R0, tsl(tt)]), (R1, mqm[1][R1, tsl(tt)])]) for tt in range(NT)], sqr, rsr)
                    m_pT, m_acc = {}, {}

                    def m_A(k):
                        tt, hh = divmod(k, 2)
                        pT = ptr_.next()
                        for mt in range(2):
                            pss = PS()
                            pe.matmul(pss[:, :], lhsT=mkT[:, hp, mt * 128:(mt + 1) * 128], rhs=mqm[hh][:, tsl(tt)],
                                      start=True, stop=True)
                            act.activation(out=pT[:, mt, :], in_=pss[:, :], func=AF.Exp, scale=0.125)
                        m_pT[k] = pT

                    def m_B(k):
                        tt, hh = divmod(k, 2)
                        h = 2 * hp + hh
                        if hh == 0:
                            m_acc[tt] = (PS(hold=True), PS(hold=True))
                        pso, psd = m_acc[tt]
                        pT = m_pT.pop(k)
                        for mt in range(2):
                            pe.matmul(pso[:, :], lhsT=mvm[:, mt, h, :], rhs=pT[:, mt, :],
                                      start=(hh == 0 and mt == 0), stop=(hh == 1 and mt == 1))
                        for mt in range(2):
                            pe.matmul(psd[:, :], lhsT=hones[hh], rhs=pT[:, mt, :],
                                      start=(hh == 0 and mt == 0), stop=(hh == 1 and mt == 1))
                        if hh == 1:
                            rc = rcr.next()
                            dve.reciprocal(out=rc, in_=psd[:, :])
                            dve.tensor_tensor(out=moT[:, hp, tsl(tt)], in0=pso[:, :], in1=rc, op=ALU.mult)
                            unhold(pso, psd)

                    m_A(0)
                    m_A(1)
                    for k in range(2 * NT):
                        if k + 2 < 2 * NT:
                            m_A(k + 2)
                        m_B(k)
                work.release(m_m)
                if l == 0:
                    tap("mo", moT)
                if stop_after == "mo":
                    raise _Stop()
                work.release(ptop)

                mergedT = work.alloc([KC, S_LEN], BF16)
                m_g = work.mark()
                acc = work.alloc([NT, TT], F32)
                sgr = Rot(work, 2, [TT], F32)
                prr = Rot(work, 2, [TT], F32)
                for dc in range(8):
                    for np_ in range(2):
                        wg_, wb_ = w_acquire(l, [("gate", dc, np_), ("br", dc, np_)])
                        wgv = wg_[:, 0:2048].rearrange("p (k n c) -> p k n c", k=8, n=2)
                        wbv = wb_[:, 0:512].rearrange("p (k n c) -> p k n c", k=2, n=2)
                        for tt in range(NT):
                            for nn in range(2):
                                n_ = 2 * np_ + nn
                                pg = PS()
                                for c in range(KC):
                                    pe.matmul(pg[:, :], lhsT=wgv[:, c, nn, :], rhs=hT[:, c, tsl(tt)],
                                              start=(c == 0), stop=(c == KC - 1))
                                pu = PS()
                                for c in range(2):
                                    pe.matmul(pu[:, :], lhsT=wbv[:, c, nn, :], rhs=br[n_][:, c, tsl(tt)],
                                              start=(c == 0), stop=(c == 1))
                                sg = sgr.next()
                                act.activation(out=sg, in_=pg[:, :], func=AF.Sigmoid)
                                if n_ == 0:
                                    dve.tensor_tensor(out=acc[:, tt, :], in0=sg, in1=pu[:, :], op=ALU.mult)
                                else:
                                    pr = prr.next()
                                    dve.tensor_tensor(out=pr, in0=sg, in1=pu[:, :], op=ALU.mult)
                                    dst = mergedT[:, dc, tsl(tt)] if n_ == 3 else acc[:, tt, :]
                                    dve.tensor_tensor(out=dst, in0=acc[:, tt, :], in1=pr, op=ALU.add)
                work.release(m_g)
                if l == 0:
                    tap("merged", mergedT)
                for j in range(4):
                    (wo,) = w_acquire(l, [("wout", j)])
                    wov = k8(wo, 256)
                    for jj in range(2):
                        dc = 2 * j + jj
                        for tt in range(NT):
                            ps = proj_fm(wov, jj * 128, tt, src=mergedT)
                            dve.tensor_tensor(out=xT[:, dc, tsl(tt)], in0=xT[:, dc, tsl(tt)], in1=ps[:, :], op=ALU.add)
                work.release(wtop)
                if l == 0:
                    tap("x1", xT)

                f_sqr = Rot(work, 2, [KC, TT], BF16)
                f_rsr = Rot(work, 2, [TT], F32)
                f_s1, f_s2 = norm_make(vo + 16, f_sqr, f_rsr)
                hid = work.alloc([FHALF, S_LEN], BF16)
                slr = Rot(work, 2, [TT], F32)
                def ffn1_tile(wfv, fi, tt):
                    pa = proj_fm(wfv, 0, tt)
                    pg = proj_fm(wfv, 128, tt)
                    sl = slr.next()
                    act.activation(out=sl, in_=pa[:, :], func=AF.Silu)
                    dve.tensor_tensor(out=hid[:, fi, tsl(tt)], in0=sl, in1=pg[:, :], op=ALU.mult)

                for half in range(2):
                    for fi in range(FHALF):
                        (wf,) = w_acquire(l, [("ffn1", half, fi)])
                        wfv = k8(wf, 256)
                        if half == 0 and fi == 0:
                            f_s1(0)
                            f_s1(1)
                            f_s2(0)
                            for tt in range(NT):
                                ffn1_tile(wfv, fi, tt)
                                if tt + 2 < NT:
                                    f_s1(tt + 2)
                                if tt + 1 < NT:
                                    f_s2(tt + 1)
                            continue
                        for tt in range(NT):
                            ffn1_tile(wfv, fi, tt)
                    for dc in range(8):
                        (w2,) = w_acquire(l, [("ffn2", half, dc)])
                        w2v = w2[:, 0:FHALF * 128].rearrange("p (k c) -> p k c", k=FHALF)
                        for tt in range(NT):
                            ps = PS()
                            for fi in range(FHALF):
                                pe.matmul(ps[:, :], lhsT=w2v[:, fi, :], rhs=hid[:, fi, tsl(tt)],
                                          start=(fi == 0), stop=(fi == FHALF - 1))
                            dve.tensor_tensor(out=xT[:, dc, tsl(tt)], in0=xT[:, dc, tsl(tt)], in1=ps[:, :], op=ALU.add)
                work.release(wtop)

        except _Stop:
            pass
        fin = []
        for c in range(KC):
            S.dma("sp", outT_d[c * 128:(c + 1) * 128, :], xT[:, c, :])
            fin.append(S.last_dma)
        for name in tap_d:
            pass
        for key in list(S.sem.keys()):
            if isinstance(key, tuple) and key[-1] == "r":
                S.eng["sp"].wait_ge(S.sem[key], S.cnt[key])
        stats = dict(n_ins=S.n_ins, n_wait=S.n_wait)
    return nc, stats


_PROG_CACHE = {}


def _get_prog(nl, taps=()):
    key = (nl, tuple(t[0] for t in taps))
    if key not in _PROG_CACHE:
        _PROG_CACHE[key] = build_program(nl, taps)
    return _PROG_CACHE[key]


def pack_weights(inp, layers):
    plan1 = layer_plan()
    chunks = []
    for l in layers:
        for (_, n, parts) in plan1:
            chunks.append(pack_slab(inp, l, parts).reshape(-1))
    return np.concatenate(chunks)


def kernel(**inputs):
    inp = {k: np.asarray(v) for k, v in inputs.items()}
    x = inp["x"].astype(np.float32, copy=False)
    mem = inp["mem"].astype(np.float32, copy=False)
    B = x.shape[0]
    cst, cbf, rope, mst = _host_consts(inp)
    wts = pack_weights(inp, range(NL))
    nc, _ = _get_prog(NL)
    in_maps = []
    for b in range(B):
        in_maps.append({
            "xT": np.ascontiguousarray(x[b].T),
            "memT": np.ascontiguousarray(mem[b].T),
            "cst": cst, "cbf": cbf, "rope": rope, "namst": mst, "wts": wts,
        })
    res = run_bass_kernel_spmd(nc, in_maps, core_ids=list(range(B)))
    out = np.stack([np.ascontiguousarray(res.results[b]["outT"].T) for b in range(B)], axis=0)
    return out.astype(np.float32, copy=False)
```

```python
import contextlib
import math
import numpy as np
import concourse.bass as bass
import concourse.mybir as mybir
from concourse.bass_utils import run_bass_kernel_spmd

F32 = mybir.dt.float32
BF16 = mybir.dt.bfloat16
ALU = mybir.AluOpType
AF = mybir.ActivationFunctionType

NCORES = 8
S_LEN = 2048
D = 1024
KC = 8
TT = 512
NT = 4
NL = 4
FF = 2816
FCH = FF // 128
FHALF = FCH // 2
EPS = 1e-6
LN8 = math.log(0.125)
SAFE_RAW = True
PERM_ROT = True
STRICT_SAME_ENGINE = True
NSLOT = 4
SLOT_ELEMS = 2048

BK = 2048


class _Rec:
    __slots__ = ("p0", "p1", "b0", "b1", "kind", "tok", "dead")


def _ap_rects(ap):
    t = ap.tensor
    row = t.shape[1]
    dsz = mybir.dt.size(ap.dtype)
    pairs = ap.ap
    pstep, pn = pairs[0]
    off = ap.offset
    p0 = off // row
    foff = off - p0 * row
    dims = sorted((abs(s), n) for s, n in pairs[1:] if n > 1 and s != 0)
    starts = [foff]
    length = 1
    for i, (s, n) in enumerate(dims):
        if s == length:
            length *= n
        elif len(starts) * n <= 64:
            starts = [st + k * s for st in starts for k in range(n)]
        else:
            ext = sum((nn - 1) * ss for ss, nn in dims[i:])
            lo = min(starts)
            hi = max(starts) + length + ext
            return t.name, p0, p0 + pn, [(lo * dsz, hi * dsz)]
    return t.name, p0, p0 + pn, [(st * dsz, (st + length) * dsz) for st in starts]


class Tracker:
    def __init__(self):
        self.idx = {}
        self.rkey = {}

    def touch(self, arena, p0, p1, runs, kind, deps):
        idx = self.idx
        for (b0, b1) in runs:
            for bk in range(b0 // BK, (b1 - 1) // BK + 1):
                lst = idx.get((arena, bk))
                if not lst:
                    continue
                keep = []
                for r in lst:
                    if r.dead:
                        continue
                    if r.p1 <= p0 or r.p0 >= p1 or r.b1 <= b0 or r.b0 >= b1:
                        keep.append(r)
                        continue
                    if kind == "r" and r.kind == "r":
                        keep.append(r)
                        continue
                    for k, v in r.tok.items():
                        if deps.get(k, 0) < v:
                            deps[k] = v
                    if kind == "w" and r.p0 >= p0 and r.p1 <= p1 and r.b0 >= b0 and r.b1 <= b1:
                        r.dead = True
                        if r.kind == "r":
                            self.rkey.pop((arena, r.p0, r.p1, r.b0, r.b1), None)
                        continue
                    keep.append(r)
                idx[(arena, bk)] = keep

    def add(self, arena, p0, p1, runs, kind, tok):
        k, v = tok
        for (b0, b1) in runs:
            if kind == "r":
                key = (arena, p0, p1, b0, b1)
                r = self.rkey.get(key)
                if r is not None and not r.dead:
                    if r.tok.get(k, 0) < v:
                        r.tok[k] = v
                    continue
            r = _Rec()
            r.p0, r.p1, r.b0, r.b1, r.kind, r.tok, r.dead = p0, p1, b0, b1, kind, {k: v}, False
            if kind == "r":
                self.rkey[(arena, p0, p1, b0, b1)] = r
            for bk in range(b0 // BK, (b1 - 1) // BK + 1):
                self.idx.setdefault((arena, bk), []).append(r)


def _is_ap(a):
    return hasattr(a, "tensor") and hasattr(a, "ap") and hasattr(a, "offset")


def _on_chip(a):
    return "DRam" not in type(a.tensor).__name__


class _EngProxy:
    def __init__(self, sched, e):
        self._s = sched
        self._e = e

    def __getattr__(self, name):
        s, e = self._s, self._e

        def call(*args, **kw):
            outs, ins = [], []
            for i, a in enumerate(args):
                if _is_ap(a):
                    (outs if i == 0 else ins).append(a)
            for k, a in kw.items():
                if _is_ap(a):
                    (outs if k in ("out", "accum_out") else ins).append(a)
            return s.op(e, lambda eng: getattr(eng, name)(*args, **kw), outs, ins)
        return call


class Sched:
    ENG = ("pe", "act", "dve", "pool", "sp")

    def __init__(self, nc, stack, safe_raw=True):
        self.nc = nc
        self.stack = stack
        self.eng = {"pe": nc.tensor, "act": nc.scalar, "dve": nc.vector, "pool": nc.gpsimd, "sp": nc.sync}
        self.sem, self.cnt, self.known, self.clocks = {}, {}, {}, {}
        for e in self.ENG:
            self.sem[e] = stack.enter_context(nc.semaphore("s_" + e))
            self.cnt[e] = 0
            self.known[e] = {}
        self.safe_raw = safe_raw
        self.trk = Tracker()
        self.ps_touch = {}
        self.n_wait = 0
        self.n_ins = 0
        self.pe = _EngProxy(self, "pe")
        self.act = _EngProxy(self, "act")
        self.dve = _EngProxy(self, "dve")
        self.pool = _EngProxy(self, "pool")

    def new_dma_sem(self, name):
        self.sem[name] = self.stack.enter_context(self.nc.semaphore("d_" + name))
        self.cnt[name] = 0
        return name

    def _collect(self, outs, ins):
        raw, oth = {}, {}
        accs = []
        for a in ins:
            if _on_chip(a):
                arena, p0, p1, runs = _ap_rects(a)
                kind = "r"
                if arena.startswith("ps"):
                    self.ps_touch[arena] = self.n_ins
                    p0, p1, runs, kind = (p0 // 32) * 32, ((p1 + 31) // 32) * 32, [(0, 2048)], "w"
                self.trk.touch(arena, p0, p1, runs, kind, raw)
                accs.append((arena, p0, p1, runs, kind))
        for a in outs:
            if _on_chip(a):
                arena, p0, p1, runs = _ap_rects(a)
                if arena.startswith("ps"):
                    self.ps_touch[arena] = self.n_ins
                    p0, p1, runs = (p0 // 32) * 32, ((p1 + 31) // 32) * 32, [(0, 2048)]
                self.trk.touch(arena, p0, p1, runs, "w", oth)
                accs.append((arena, p0, p1, runs, "w"))
        return raw, oth, accs

    def _do_waits(self, e, raw, oth, is_dma=False):
        known = self.known[e]
        eng = self.eng[e]
        deps = dict(oth)
        for k, v in raw.items():
            if deps.get(k, 0) < v:
                deps[k] = v
        for k, v in deps.items():
            if k == e and not is_dma:
                if e == "pe" or not self.safe_raw:
                    continue
                if not STRICT_SAME_ENGINE:
                    v = raw.get(k, 0)
                    if v == 0:
                        continue
            if known.get(k, 0) >= v:
                continue
            eng.wait_ge(self.sem[k], v)
            self.n_wait += 1
            known[k] = v
            clk = self.clocks.get((k, v))
            if clk:
                for kk, vv in clk.items():
                    if known.get(kk, 0) < vv:
                        known[kk] = vv

    def op(self, e, fn, outs, ins):
        raw, oth, accs = self._collect(outs, ins)
        self._do_waits(e, raw, oth)
        ins_ = fn(self.eng[e])
        self.cnt[e] += 1
        v = self.cnt[e]
        ins_.then_inc(self.sem[e], 1)
        self.n_ins += 1
        self.clocks[(e, v)] = dict(self.known[e])
        for (arena, p0, p1, runs, kind) in accs:
            self.trk.add(arena, p0, p1, runs, kind, (e, v))
        return ins_

    def dma(self, e, out, in_):
        outs = [out] if _on_chip(out) else []
        ins = [in_] if _on_chip(in_) else []
        chip = out if _on_chip(out) else in_
        arena, p0, p1, runs = _ap_rects(chip)
        key = ("dma", e, arena, p0, runs[0][0], "w" if _on_chip(out) else "r")
        if key not in self.sem:
            self.sem[key] = self.stack.enter_context(self.nc.semaphore("d%d" % len(self.sem)))
            self.cnt[key] = 0
        raw, oth, accs = self._collect(outs, ins)
        self._do_waits(e, raw, oth, is_dma=True)
        ins_ = self.eng[e].dma_start(out=out, in_=in_)
        self.cnt[key] += 16
        v = self.cnt[key]
        ins_.then_inc(self.sem[key], 16)
        self.clocks[(key, v)] = dict(self.known[e])
        for (arena, p0, p1, runs, kind) in accs:
            self.trk.add(arena, p0, p1, runs, kind, (key, v))
        self.last_dma = (key, v)
        return ins_


W_IN_BASE = dict(rq=0, rk=256, rv=512, rg=768, pv=1024, nq=1280, nk=1536, nv=1792, mq=2048)
_SWAP = np.array([(p // 64) * 64 + ((p % 64) + 32) % 64 for p in range(128)])
_AR = np.arange(128)


def _cols(name, hp, swap=False):
    return W_IN_BASE[name] + hp * 128 + (_SWAP if swap else _AR)


def layer_plan():
    plan = []
    allr = (0, 1024)

    def add(key, parts):
        n = sum((p[3] // 128) * len(p[4]) for p in parts) if False else None
        kr = parts[0][3] // 128
        n = kr * sum(len(p[4]) for p in parts)
        assert n <= SLOT_ELEMS, (key, n)
        plan.append((key, n, parts))

    for hp in range(2):
        add(("rqk", hp), [("w_in", None, 0, 1024, np.concatenate([_cols("rq", hp), _cols("rk", hp)]))])
        add(("rgv", hp), [("w_in", None, 0, 1024, np.concatenate([_cols("rg", hp), _cols("rv", hp)]))])
    add("pv", [("w_in", None, 0, 1024, np.arange(1024, 1280))])
    add("memk", [("w_mem_kv", None, 0, 1024, np.arange(0, 256))])
    add("memv", [("w_mem_kv", None, 0, 1024, np.arange(256, 512))])
    for hp in range(2):
        add(("nqk", hp), [("w_in", None, 0, 1024, np.concatenate([_cols("nq", hp), _cols("nk", hp)]))])
        add(("nv", hp), [("w_in", None, 0, 1024, _cols("nv", hp))])
    add("mq", [("w_in", None, 0, 1024, np.arange(2048, 2304))])
    for dc in range(8):
        for np_ in range(2):
            gcols = np.concatenate([(2 * np_ + nn) * 1024 + dc * 128 + _AR for nn in range(2)])
            add(("gate", dc, np_), [("w_gate", None, 0, 1024, gcols)])
            add(("br", dc, np_), [("w_branch", 2 * np_ + nn, 0, 256, dc * 128 + _AR) for nn in range(2)])
    for j in range(4):
        add(("wout", j), [("w_out", None, 0, 1024, np.arange(j * 256, (j + 1) * 256))])
    for half in range(2):
        for fi in range(FHALF):
            f = half * FHALF + fi
            add(("ffn1", half, fi), [("w_ffn_in", None, 0, 1024, np.concatenate([f * 128 + _AR, FF + f * 128 + _AR]))])
        for dc in range(8):
            add(("ffn2", half, dc), [("w_ffn_out", None, half * FHALF * 128, FHALF * 128, dc * 128 + _AR)])
    return plan


def pack_slab(weights, l, parts):
    blocks = []
    for (name, sub, r0, nr, cols) in parts:
        w = weights[name][l]
        if sub is not None:
            w = w[sub]
        blk = w[r0:r0 + nr][:, cols]
        blk = blk.reshape(nr // 128, 128, len(cols)).transpose(1, 0, 2)
        blocks.append(blk)
    if len(blocks) == 1:
        out = blocks[0]
    else:
        out = np.stack(blocks, axis=2)
    return np.ascontiguousarray(out.reshape(128, -1), dtype=np.float32)


C_P, C_Q, C_I1, C_I2, C_J1, C_J2, C_VEC = 0, 128, 256, 384, 512, 513, 514
VEC_W = 32
C_DEC = C_VEC + VEC_W * NL
NCST = C_DEC + 8 * NL
B_ID, B_ONE, B_BLK, B_CM, B_BAND = 0, 128, 256, 384, 448
B_PW = B_BAND + 20 * 128
B_PERM = B_PW + 128 * NL
B_H0 = B_PERM + 128
B_H1 = B_H0 + 128
NCBF = B_H1 + 128


def _pool_bands():
    S = S_LEN
    out = np.zeros((4, 5, 128, 128), np.float32)
    t = np.arange(S)
    for g, Wd in enumerate((2, 4, 8, 16)):
        half = Wd // 2
        lo = np.clip(t - half, 0, S)
        hi = np.clip(t + half, 0, S)

        def block(j, jm):
            B = np.zeros((128, 128), np.float64)
            for ti in range(128):
                tg = j * 128 + ti
                for m in range(max(lo[tg], jm * 128), min(hi[tg], (jm + 1) * 128)):
                    B[ti, m - jm * 128] += 1.0 / (hi[tg] - lo[tg])
                if jm == j:
                    B[ti, ti] -= 1.0
            return B.T.astype(np.float32)
        out[g, 0] = block(5, 4)
        out[g, 1] = block(5, 6)
        out[g, 2] = block(0, 0)
        out[g, 3] = block(5, 5)
        out[g, 4] = block(15, 15)
    return out


def _host_consts(inp):
    cst = np.zeros((128, NCST), np.float32)
    m = np.arange(128)[:, None]
    n = np.arange(128)[None, :]
    cst[:, C_P:C_P + 128] = np.maximum(n - m, 0)
    cst[:, C_Q:C_Q + 128] = np.maximum(m - n, 0)
    cst[:, C_I1:C_I1 + 128] = n + 1
    cst[:, C_I2:C_I2 + 128] = 128 - n
    cst[:, C_J1] = 127 - np.arange(128)
    cst[:, C_J2] = np.arange(128)
    p = np.arange(128)
    for l in range(NL):
        o = C_VEC + VEC_W * l
        cst[:, o:o + 8] = inp["norm_mix_g"][l].reshape(8, 128).T
        cst[:, o + 8:o + 16] = inp["norm_mem_g"][l].reshape(8, 128).T
        cst[:, o + 16:o + 24] = inp["norm_ffn_g"][l].reshape(8, 128).T
        cst[:, o + 24:o + 26] = inp["ret_norm_g"][l].reshape(2, 128).T
        cst[:, o + 26:o + 28] = inp["pool_scale"][l].reshape(2, 128).T
        cst[:, o + 28] = inp["na_q_norm_g"][l][p % 64]
        cst[:, o + 29] = inp["na_k_norm_g"][l][p % 64]
        cst[:, o + 30] = inp["mem_q_norm_g"][l][p % 64]
        cst[:, o + 31] = inp["mem_k_norm_g"][l][p % 64]
        cst[:, C_DEC + 8 * l:C_DEC + 8 * l + 4] = inp["ret_decay_fwd"][l][None, :]
        cst[:, C_DEC + 8 * l + 4:C_DEC + 8 * l + 8] = inp["ret_decay_bwd"][l][None, :]
    cbf = np.zeros((128, NCBF), np.float32)
    cbf[:, B_ID:B_ID + 128] = np.eye(128)
    cbf[:, B_ONE:B_ONE + 128] = 1.0
    blk = np.zeros((128, 128), np.float32)
    blk[:64, :64] = 1.0
    blk[64:, 64:] = 1.0
    cbf[:, B_BLK:B_BLK + 128] = blk
    kc = (p % 64)[:, None]
    q = np.arange(64)[None, :]
    qwin = np.clip(q - 8, 0, 48)
    cbf[:, B_CM:B_CM + 64] = np.where((kc >= qwin) & (kc < qwin + 16), 0.0, -1e30)
    bands = _pool_bands()
    cbf[:, B_BAND:B_BAND + 20 * 128] = bands.reshape(20, 128, 128).transpose(1, 0, 2).reshape(128, -1)
    for l in range(NL):
        pw = inp["pool_w"][l]
        blkw = np.zeros((128, 2, 64), np.float32)
        for g in range(4):
            blkw[(g % 2) * 64:(g % 2) * 64 + 64, g // 2, :] = pw[g]
        cbf[:, B_PW + 128 * l:B_PW + 128 * (l + 1)] = blkw.reshape(128, 128)
    cbf[_SWAP, B_PERM + np.arange(128)] = 1.0
    cbf[:, B_H0:B_H0 + 64] = 1.0
    cbf[:, B_H1 + 64:B_H1 + 128] = 1.0
    half = 32
    inv = (10000.0 ** (-np.arange(half, dtype=np.float32) / half)).astype(np.float32)
    ang = (np.arange(S_LEN, dtype=np.float32)[None, :] * inv[:, None]).astype(np.float32)
    d = p % 64
    rope = np.zeros((2, 128, S_LEN), np.float32)
    rope[0] = np.cos(ang)[d % 32]
    sgn = np.where(d < 32, -1.0, 1.0).astype(np.float32)[:, None]
    rope[1] = np.sin(ang)[d % 32] * sgn
    mst = np.zeros((NL, 128, 2, 2, 8, 2, 64), np.float32)
    kcol = (p % 64)[:, None]
    qq = np.arange(64)[None, :]
    coff = np.clip(kcol - qq, -15, 15) + 15
    qwin_ = np.clip(qq - 8, 0, 48)
    cmask_ok = (kcol >= qwin_) & (kcol < qwin_ + 16)
    for par in range(2):
        for s in range(8):
            roff = 2 * s + (p // 64) - (8 if par == 0 else 7)
            ok = (roff >= -7) & (roff <= 7)
            ridx = np.clip(roff, -7, 7) + 7
            for l in range(NL):
                for h in range(4):
                    vals = inp["na_rpb"][l][h][ridx[:, None], coff]
                    mst[l, :, par, h // 2, s, h % 2, :] = np.where(ok[:, None] & cmask_ok, vals, -1e30)
    return cst, cbf, rope, mst.reshape(NL, 128, -1)


class Arena:
    def __init__(self, big, base, size):
        self.big, self.base, self.size, self.top = big, base, size, 0

    def alloc(self, shape, dtype):
        n = int(np.prod(shape))
        nb = n * mybir.dt.size(dtype)
        nb_al = (nb + 63) // 64 * 64
        assert self.top + nb_al <= self.size, ("arena overflow", self.top, nb_al, self.size)
        off = self.base + self.top
        self.top += nb_al
        v = self.big[:, off // 4:(off + nb_al) // 4]
        if dtype != F32:
            v = v.bitcast(dtype)
        v = v[:, 0:n]
        if len(shape) == 2:
            v = v.rearrange("p (a b) -> p a b", a=shape[0])
        elif len(shape) == 3:
            v = v.rearrange("p (a b c) -> p a b c", a=shape[0], b=shape[1])
        elif len(shape) == 4:
            v = v.rearrange("p (a b c d) -> p a b c d", a=shape[0], b=shape[1], c=shape[2])
        elif len(shape) == 5:
            v = v.rearrange("p (a b c d e) -> p a b c d e", a=shape[0], b=shape[1], c=shape[2], d=shape[3])
        return v

    def mark(self):
        return self.top

    def release(self, m):
        self.top = m


class Rot:
    def __init__(self, arena, n, shape, dtype):
        self.bufs = [arena.alloc(shape, dtype) for _ in range(n)]
        self.i = 0

    def next(self):
        b = self.bufs[self.i % len(self.bufs)]
        self.i += 1
        return b


class _Stop(Exception):
    pass


def build_program(nl, taps=(), stop_after=None):
    nc = bass.Bass("TRN2", target_bir_lowering=False)
    plan1 = layer_plan()
    n_per_layer = sum(128 * n for _, n, _ in plan1)
    xT_d = nc.dram_tensor("xT", [D, S_LEN], F32, kind="ExternalInput").ap()
    memT_d = nc.dram_tensor("memT", [D, 256], F32, kind="ExternalInput").ap()
    cst_d = nc.dram_tensor("cst", [128, NCST], F32, kind="ExternalInput").ap()
    cbf_d = nc.dram_tensor("cbf", [128, NCBF], F32, kind="ExternalInput").ap()
    rope_d = nc.dram_tensor("rope", [2, 128, S_LEN], F32, kind="ExternalInput").ap()
    mst_d = nc.dram_tensor("namst", [NL, 128, 4096], F32, kind="ExternalInput").ap()
    wts_d = nc.dram_tensor("wts", [nl * n_per_layer], F32, kind="ExternalInput").ap()
    outT_d = nc.dram_tensor("outT", [D, S_LEN], F32, kind="ExternalOutput").ap()
    tap_d = {}
    for name, shape, dt in taps:
        tap_d[name] = nc.dram_tensor("tap_" + name, [128] + list(shape), dt, kind="ExternalOutput").ap()

    TOTAL = 209920
    big = nc.alloc_sbuf_tensor("big", [128, TOTAL // 4], F32)
    psb = [nc.alloc_psum_tensor(f"ps{i}", [128, 512], F32) for i in range(8)]
    OFF_X = 0
    OFF_H = 65536
    OFF_RING = OFF_H + 32768
    OFF_CST = OFF_RING + NSLOT * SLOT_ELEMS * 2
    CST_BYTES = 11776
    OFF_WORK = OFF_CST + CST_BYTES
    WORK_BYTES = TOTAL - OFF_WORK
    xT = big[:, OFF_X // 4:(OFF_X + 65536) // 4].rearrange("p (c t) -> p c t", c=KC)
    hT = big[:, OFF_H // 4:(OFF_H + 32768) // 4].bitcast(BF16).rearrange("p (c t) -> p c t", c=KC)
    ring = [big[:, (OFF_RING + i * 4096) // 4:(OFF_RING + (i + 1) * 4096) // 4].bitcast(BF16) for i in range(NSLOT)]
    carena = Arena(big, OFF_CST, CST_BYTES)
    work = Arena(big, OFF_WORK, WORK_BYTES)
    cst = carena.alloc([NCST], F32)
    cbf = carena.alloc([NCBF], BF16)
    rstd_mem = carena.alloc([256], F32)
    lgall = carena.alloc([8 * NL], F32)
    ident = cbf[:, B_ID:B_ID + 128]
    ones = cbf[:, B_ONE:B_ONE + 128]
    blkones = cbf[:, B_BLK:B_BLK + 128]

    def tsl(tt):
        return slice(tt * TT, (tt + 1) * TT)

    with contextlib.ExitStack() as stack:
        S = Sched(nc, stack, safe_raw=SAFE_RAW)
        pe, act, dve = S.pe, S.act, S.dve

        psi = [0]
        held = set()

        def PS(hold=False):
            cand = [i for i in range(8) if i not in held]
            i = min(cand, key=lambda j: (S.ps_touch.get("ps%d" % j, -1), (j - psi[0]) % 8))
            psi[0] = i + 1
            S.ps_touch["ps%d" % i] = S.n_ins
            if hold:
                held.add(i)
            return psb[i]

        def unhold(*pss):
            for p_ in pss:
                held.discard(psb.index(p_))

        def tap(name, ap):
            if name in tap_d:
                S.dma("sp", tap_d[name], ap)

        full_plan = [(l, k, n, parts) for l in range(nl) for (k, n, parts) in plan1]
        offs = np.cumsum([0] + [128 * n for (_, _, n, _) in full_plan])
        ws = dict(issue=0, use=0)

        def w_issue(j):
            l, k, n, parts = full_plan[j]
            slot = j % NSLOT
            src = wts_d[int(offs[j]):int(offs[j]) + 128 * n].rearrange("(p l) -> p l", p=128)
            S.dma("pool", ring[slot][:, 0:n], src)

        def w_acquire(l, keys):
            first = ws["use"]
            views = []
            for i, k in enumerate(keys):
                fl, fk, fn_, _ = full_plan[first + i]
                assert fl == l and fk == k, (fl, fk, l, k)
                views.append(ring[(first + i) % NSLOT])
            ws["use"] += len(keys)
            limit = min(first + NSLOT - 1, len(full_plan) - 1)
            while ws["issue"] <= limit:
                w_issue(ws["issue"])
                ws["issue"] += 1
            return views

        def k8(view, ncols):
            return view[:, 0:8 * ncols].rearrange("p (k c) -> p k c", k=8)

        S.dma("pool", cbf, cbf_d)
        S.dma("sp", cst, cst_d)
        for tt in range(NT):
            for c in range(KC):
                S.dma("sp", xT[:, c, tsl(tt)], xT_d[c * 128:(c + 1) * 128, tsl(tt)])

        dec = cst[:, C_DEC:C_DEC + 8 * nl]
        act.activation(out=lgall[:, 0:8 * nl], in_=dec, func=AF.Exp, scale=-1.0)
        act.activation(out=lgall[:, 0:8 * nl], in_=lgall[:, 0:8 * nl], func=AF.Ln, bias=1.0)
        act.mul(out=lgall[:, 0:8 * nl], in_=lgall[:, 0:8 * nl], mul=-1.0)

        def norm_make(gcol0, sqr, rsr):
            st = {}

            def s1(tt):
                sq = sqr.next()
                act.activation(out=sq, in_=xT[:, :, tsl(tt)], func=AF.Square)
                ps = PS(hold=True)
                for c in range(KC):
                    pe.matmul(ps[:, :], lhsT=ones, rhs=sq[:, c, :], start=(c == 0), stop=(c == KC - 1))
                st[tt] = ps

            def s2(tt):
                ps = st.pop(tt)
                rs = rsr.next()
                act.activation(out=rs, in_=ps[:, :], func=AF.Ln, scale=1.0 / D, bias=EPS)
                unhold(ps)
                act.activation(out=rs, in_=rs, func=AF.Exp, scale=-0.5)
                for c in range(KC):
                    dve.scalar_tensor_tensor(out=hT[:, c, tsl(tt)], in0=xT[:, c, tsl(tt)],
                                             scalar=cst[:, gcol0 + c:gcol0 + c + 1], in1=rs,
                                             op0=ALU.mult, op1=ALU.mult)
            return s1, s2

        def proj_fm(wv, col0, tt, src=None):
            src = hT if src is None else src
            ps = PS()
            for c in range(KC):
                pe.matmul(ps[:, :], lhsT=wv[:, c, col0:col0 + 128], rhs=src[:, c, tsl(tt)],
                          start=(c == 0), stop=(c == KC - 1))
            return ps

        def qk_run(items, sqr, rsr, n=TT):
            st = {}

            def s1(i):
                ps = items[i][0]()
                sq = sqr.next()
                act.activation(out=sq[:, 0:n], in_=ps[:, 0:n], func=AF.Square)
                st[i] = (ps, sq)

            def s2(i):
                ps, sq = st.pop(i)
                _, gcol, outs = items[i]
                pn = PS()
                pe.matmul(pn[:, 0:n], lhsT=blkones, rhs=sq[:, 0:n], start=True, stop=True)
                rs = rsr.next()
                act.activation(out=rs[:, 0:n], in_=pn[:, 0:n], func=AF.Ln, scale=1.0 / 64, bias=EPS)
                act.activation(out=rs[:, 0:n], in_=rs[:, 0:n], func=AF.Exp, scale=-0.5)
                for rows, out_ap in outs:
                    i0, i1 = ps[rows, 0:n], rs[rows, 0:n]
                    if len(out_ap.shape) == 3:
                        i0 = i0.rearrange("p (a b) -> p a b", a=out_ap.shape[1])
                        i1 = i1.rearrange("p (a b) -> p a b", a=out_ap.shape[1])
                    gsc = cst[rows, gcol:gcol + 1] if isinstance(gcol, int) else gcol[rows, 0:1]
                    dve.scalar_tensor_tensor(out=out_ap, in0=i0, scalar=gsc,
                                             in1=i1, op0=ALU.mult, op1=ALU.mult)
            s1(0)
            for i in range(len(items)):
                if i + 1 < len(items):
                    s1(i + 1)
                s2(i)

        ALLR = slice(0, 128)
        R0 = slice(0, 64)
        R1 = slice(64, 128)
        perm = cbf[:, B_PERM:B_PERM + 128]
        hones = [cbf[:, B_H0:B_H0 + 128], cbf[:, B_H1:B_H1 + 128]]

        try:
            for l in range(nl):
                vo = C_VEC + VEC_W * l
                wtop = work.mark()
                m_n = work.mark()
                n_sqr = Rot(work, 2, [KC, TT], BF16)
                n_rsr = Rot(work, 2, [TT], F32)
                work.release(m_n)
                n_s1, n_s2 = norm_make(vo + 0, n_sqr, n_rsr)
                n_s1(0)
                n_s1(1)
                n_s2(0)

                def norm_hook(i):
                    if i + 2 < NT:
                        n_s1(i + 2)
                    n_s2(i + 1)
                br = [work.alloc([2, S_LEN], BF16) for _ in range(4)]
                retT, poolT, naT, moT = br
                ptop = work.mark()

                for hp in range(2):
                    m_hp = work.mark()
                    krot = work.alloc([S_LEN], BF16)
                    qblk = work.alloc([16, 2, 128], BF16)
                    DT = work.alloc([2, 128], F32)
                    lgc = work.alloc([4], F32)
                    qfall = work.alloc([2, S_LEN], BF16)
                    cf = 8 * l + 2 * hp
                    cb = 8 * l + 4 + 2 * hp
                    S.pool.memset(qblk[R0, :, 1, :], 0.0)
                    S.pool.memset(qblk[R1, :, 0, :], 0.0)

                    m_rot = work.mark()
                    QD = work.alloc([2, 128], F32)
                    csr = Rot(work, 3, [2, TT], F32)
                    tar = Rot(work, 2, [TT], F32)
                    tbr = Rot(work, 2, [TT], F32)
                    qbr = Rot(work, 2, [TT], BF16)
                    qrot = work.alloc([S_LEN], BF16)
                    (wq,) = w_acquire(l, [("rqk", hp)])
                    wqv = k8(wq, 256)
                    rot_st = {}

                    def rot_1(i):
                        col0, tt = (0, i) if i < NT else (128, i - NT)
                        cs_ = csr.next()
                        S.dma("sp", cs_[:, 0, :], rope_d[0, :, tsl(tt)])
                        S.dma("sp", cs_[:, 1, :], rope_d[1, :, tsl(tt)])
                        pa = proj_fm(wqv, col0, tt)
                        qb = qbr.next()
                        act.copy(out=qb, in_=pa[:, :])
                        rot_st[i] = (cs_, pa, qb)

                    def rot_2(i):
                        dst, tt = (qrot, i) if i < NT else (krot, i - NT)
                        cs_, pa, qb = rot_st.pop(i)
                        pb = PS()
                        pe.matmul(pb[:, :], lhsT=perm, rhs=qb, start=True, stop=True)
                        ta = tar.next()
                        tb = tbr.next()
                        dve.tensor_tensor(out=ta, in0=pa[:, :], in1=cs_[:, 0, :], op=ALU.mult)
                        dve.tensor_tensor(out=tb, in0=pb[:, :], in1=cs_[:, 1, :], op=ALU.mult)
                        dve.tensor_tensor(out=dst[:, tsl(tt)], in0=ta, in1=tb, op=ALU.add)
                        if i < NT:
                            for hh, rows in ((0, R0), (1, R1)):
                                act.copy(out=qblk[rows, 4 * tt:4 * tt + 4, hh, :],
                                         in_=qrot[rows, tsl(tt)].rearrange("p (a b) -> p a b", a=4))
                    rot_1(0)
                    if hp == 0:
                        norm_hook(0)
                    for i in range(2 * NT):
                        if i + 1 < 2 * NT:
                            rot_1(i + 1)
                            if hp == 0 and i + 1 < NT - 1:
                                norm_hook(i + 1)
                        rot_2(i)
                    if hp == 0 and l == 0:
                        tap("h", hT[:, :, :])
                    for hh in range(2):
                        rows = slice(hh * 64, (hh + 1) * 64)
                        dve.tensor_copy(out=lgc[rows, 0:1], in_=lgall[rows, cf + hh:cf + hh + 1])
                        dve.tensor_copy(out=lgc[rows, 1:2], in_=lgall[rows, cb + hh:cb + hh + 1])
                    act.activation(out=lgc[:, 2:4], in_=lgc[:, 0:2], func=AF.Exp, scale=128.0)
                    for hh in range(2):
                        dve.tensor_scalar(out=DT[:, hh, :], in0=cst[:, C_P:C_P + 128],
                                          scalar1=lgall[:, cf + hh:cf + hh + 1], scalar2=None, op0=ALU.mult)
                        dve.scalar_tensor_tensor(out=DT[:, hh, :], in0=cst[:, C_Q:C_Q + 128],
                                                 scalar=lgall[:, cb + hh:cb + hh + 1], in1=DT[:, hh, :],
                                                 op0=ALU.mult, op1=ALU.add)
                    act.activation(out=DT, in_=DT, func=AF.Exp, bias=LN8)
                    act.activation(out=QD[:, 0, :], in_=cst[:, C_I1:C_I1 + 128], func=AF.Exp, scale=lgc[:, 0:1], bias=LN8)
                    act.activation(out=QD[:, 1, :], in_=cst[:, C_I2:C_I2 + 128], func=AF.Exp, scale=lgc[:, 1:2], bias=LN8)
                    for tt in range(NT):
                        for fb in range(2):
                            S.pool.tensor_tensor(out=qfall[:, fb, tsl(tt)].rearrange("p (a b) -> p a b", a=4),
                                                 in0=qrot[:, tsl(tt)].rearrange("p (a b) -> p a b", a=4),
                                                 in1=QD[:, fb, :].unsqueeze(1).to_broadcast([128, 4, 128]),
                                                 op=ALU.mult)
                    work.release(m_rot)

                    silu_rg = work.alloc([S_LEN], BF16)
                    vtok = work.alloc([16, 128], BF16)
                    m_sb = work.mark()
                    kdr = Rot(work, 2, [2, 4, 128], BF16)
                    work.release(m_sb)
                    st_blk = work.alloc([16, 2, 128], BF16)
                    (wg,) = w_acquire(l, [("rgv", hp)])
                    wgv = k8(wg, 256)
                    for g4 in range(4):
                        ps = PS()
                        for i in range(4):
                            j = g4 * 4 + i
                            for c in range(KC):
                                pe.matmul(ps[:, i * 128:(i + 1) * 128], lhsT=hT[:, c, j * 128:(j + 1) * 128],
                                          rhs=wgv[:, c, 128:256], start=(c == 0), stop=(c == KC - 1))
                        act.copy(out=vtok[:, g4 * 4:(g4 + 1) * 4, :], in_=ps[:, :].rearrange("p (a b) -> p a b", a=4))

                    m_kv = work.mark()
                    KD = work.alloc([2, 128], F32)
                    act.activation(out=KD[:, 0, :].rearrange("p (h d) -> p h d", h=2),
                                   in_=lgall[:, cf:cf + 2].unsqueeze(2).to_broadcast([128, 2, 64]),
                                   func=AF.Exp, scale=cst[:, C_J1:C_J1 + 1])
                    act.activation(out=KD[:, 1, :].rearrange("p (h d) -> p h d", h=2),
                                   in_=lgall[:, cb:cb + 2].unsqueeze(2).to_broadcast([128, 2, 64]),
                                   func=AF.Exp, scale=cst[:, C_J2:C_J2 + 1])
                    kvs = work.alloc([16, 2, 64], F32)

                    def kv_A(g4):
                        pt = PS()
                        ptb = pt[:, 0:256].bitcast(BF16).rearrange("p (a b) -> p a b", a=4)
                        for i in range(4):
                            c_ = g4 * 4 + i
                            pe.transpose(out=ptb[:, i, :], in_=krot[:, c_ * 128:(c_ + 1) * 128], identity=ident)
                        kd = kdr.next()
                        for fb in range(2):
                            dve.tensor_tensor(out=kd[:, fb, :, :], in0=ptb,
                                              in1=KD[:, fb, :].unsqueeze(1).to_broadcast([128, 4, 128]), op=ALU.mult)
                        return kd

                    def kv_B(g4, kd):
                        for half in range(2):
                            pk = PS()
                            pkv = pk[:, :].rearrange("p (i f e) -> p i f e", i=2, f=2)
                            for ii in range(2):
                                i = half * 2 + ii
                                c_ = g4 * 4 + i
                                for fb in range(2):
                                    pe.matmul(pkv[:, ii, fb, :], lhsT=kd[:, fb, i, :], rhs=vtok[:, c_, :],
                                              start=True, stop=True)
                            c0 = g4 * 4 + half * 2
                            act.copy(out=kvs[R0, c0:c0 + 2, :, :], in_=pkv[R0, :, :, 0:64])
                            act.copy(out=kvs[R1, c0:c0 + 2, :, :], in_=pkv[R1, :, :, 64:128])
                    kds = {0: kv_A(0)}
                    for g4 in range(4):
                        if g4 + 1 < 4:
                            kds[g4 + 1] = kv_A(g4 + 1)
                        kv_B(g4, kds[g4])
                    for tt in range(NT):
                        pg = proj_fm(wgv, 0, tt)
                        act.activation(out=silu_rg[:, tsl(tt)], in_=pg[:, :], func=AF.Silu)
                    for j_ in range(1, 16):
                        c_ = j_
                        dve.scalar_tensor_tensor(out=kvs[:, c_, 0, :], in0=kvs[:, c_ - 1, 0, :], scalar=lgc[:, 2:3],
                                                 in1=kvs[:, c_, 0, :], op0=ALU.mult, op1=ALU.add)
                        c_ = 15 - j_
                        dve.scalar_tensor_tensor(out=kvs[:, c_, 1, :], in0=kvs[:, c_ + 1, 1, :], scalar=lgc[:, 3:4],
                                                 in1=kvs[:, c_, 1, :], op0=ALU.mult, op1=ALU.add)
                    S.pool.memset(st_blk[R0, :, :, 64:128], 0.0)
                    S.pool.memset(st_blk[R1, :, :, 0:64], 0.0)
                    act.copy(out=st_blk[R0, :, :, 0:64], in_=kvs[R0, :, :, :])
                    dve.tensor_copy(out=st_blk[R1, :, :, 64:128], in_=kvs[R1, :, :, :])
                    work.release(m_kv)

                    m_o = work.mark()
                    ptr_ = Rot(work, 2, [4, 2, 128], BF16)
                    sqr = Rot(work, 2, [TT], BF16)
                    rsr = Rot(work, 2, [TT], F32)
                    tmr = Rot(work, 1, [TT], F32)
                    pTs, pos_ = {}, {}

                    def out_A(g4):
                        pT = ptr_.next()
                        for half in range(2):
                            pss = PS()
                            for ii in range(2):
                                c_ = g4 * 4 + half * 2 + ii
                                pe.matmul(pss[:, ii * 256:(ii + 1) * 256], lhsT=krot[:, c_ * 128:(c_ + 1) * 128],
                                          rhs=qblk[:, c_, :, :].rearrange("p g n -> p (g n)"), start=True, stop=True)
                            dve.tensor_tensor(out=pT[:, half * 2:half * 2 + 2, :, :],
                                              in0=pss[:, :].rearrange("p (c g n) -> p c g n", c=2, g=2),
                                              in1=DT[:, :, :].unsqueeze(1).to_broadcast([128, 2, 2, 128]), op=ALU.mult)
                        pTs[g4] = pT

                    def out_B(g4):
                        pT = pTs.pop(g4)
                        regs = [PS(hold=True), PS(hold=True)]
                        for i in range(4):
                            c_ = g4 * 4 + i
                            has_f = c_ > 0
                            has_b = c_ < 15
                            tokc = slice(c_ * 128, (c_ + 1) * 128)
                            for hh in range(2):
                                o_ap = regs[hh][:, i * 128:(i + 1) * 128]
                                pe.matmul(o_ap, lhsT=vtok[:, c_, :], rhs=pT[:, i, hh, :], start=True,
                                          stop=not (has_f or has_b))
                                if has_f:
                                    pe.matmul(o_ap, lhsT=st_blk[:, c_ - 1, 0, :], rhs=qfall[:, 0, tokc],
                                              start=False, stop=not has_b)
                                if has_b:
                                    pe.matmul(o_ap, lhsT=st_blk[:, c_ + 1, 1, :], rhs=qfall[:, 1, tokc],
                                              start=False, stop=True)
                        pos_[g4] = regs

                    def out_C(g4):
                        tok4 = slice(g4 * 512, (g4 + 1) * 512)
                        po = pos_.pop(g4)
                        sq = sqr.next()
                        for hh, rows in ((0, R0), (1, R1)):
                            act.activation(out=sq[rows, :], in_=po[hh][rows, :], func=AF.Square)
                        pn = PS()
                        pe.matmul(pn[:, :], lhsT=blkones, rhs=sq, start=True, stop=True)
                        rs = rsr.next()
                        act.activation(out=rs, in_=pn[:, :], func=AF.Ln, scale=1.0 / 64, bias=EPS)
                        act.activation(out=rs, in_=rs, func=AF.Exp, scale=-0.5)
                        tm = tmr.next()
                        for hh, rows in ((0, R0), (1, R1)):
                            dve.scalar_tensor_tensor(out=tm[rows, :], in0=po[hh][rows, :],
                                                     scalar=cst[rows, vo + 24 + hp:vo + 25 + hp], in1=rs[rows, :],
                                                     op0=ALU.mult, op1=ALU.mult)
                        dve.tensor_tensor(out=retT[:, hp, tok4], in0=tm, in1=silu_rg[:, tok4], op=ALU.mult)
                        unhold(*po)

                    out_A(0)
                    for g4 in range(4):
                        if g4 + 1 < 4:
                            out_A(g4 + 1)
                        out_B(g4)
                        if g4 >= 1:
                            out_C(g4 - 1)
                    out_C(3)
                    work.release(m_o)
                    work.release(m_hp)
                if l == 0:
                    tap("ret", retT)
                if stop_after == "ret":
                    raise _Stop()

                mkT = work.alloc([2, 256], BF16)
                mvm = work.alloc([2, 4, 128], BF16)
                mtop = work.mark()
                memx = work.alloc([KC, 256], F32)
                for c in range(KC):
                    S.dma("sp", memx[:, c, :], memT_d[c * 128:(c + 1) * 128, :])
                if l == 0:
                    msq = work.alloc([KC, 256], BF16)
                    act.activation(out=msq, in_=memx, func=AF.Square)
                    ps = PS()
                    for c in range(KC):
                        pe.matmul(ps[:, 0:256], lhsT=ones, rhs=msq[:, c, :], start=(c == 0), stop=(c == KC - 1))
                    act.activation(out=rstd_mem, in_=ps[:, 0:256], func=AF.Ln, scale=1.0 / D, bias=EPS)
                    act.activation(out=rstd_mem, in_=rstd_mem, func=AF.Exp, scale=-0.5)
                memn = work.alloc([KC, 256], BF16)
                for c in range(KC):
                    dve.scalar_tensor_tensor(out=memn[:, c, :], in0=memx[:, c, :], scalar=cst[:, vo + 8 + c:vo + 9 + c],
                                             in1=rstd_mem, op0=ALU.mult, op1=ALU.mult)
                m_p = work.mark()
                pvt = work.alloc([16, 256], BF16)
                (wp,) = w_acquire(l, ["pv"])
                wpv = k8(wp, 256)
                for g2 in range(8):
                    ps = PS()
                    for i in range(2):
                        j = g2 * 2 + i
                        for c in range(KC):
                            pe.matmul(ps[:, i * 256:(i + 1) * 256], lhsT=hT[:, c, j * 128:(j + 1) * 128],
                                      rhs=wpv[:, c, :], start=(c == 0), stop=(c == KC - 1))
                    act.copy(out=pvt[:, g2 * 2:(g2 + 1) * 2, :], in_=ps[:, :].rearrange("p (a b) -> p a b", a=2))
                pldr = Rot(work, 2, [TT], BF16)
                pw = cbf[:, B_PW + 128 * l:B_PW + 128 * (l + 1)].rearrange("p (a b) -> p a b", a=2)

                def band(g, v):
                    o = B_BAND + (g * 5 + v) * 128
                    return cbf[:, o:o + 128]
                plds = {}

                def pool_A(k):
                    gp, tg = divmod(k, 4)
                    psp = PS()
                    for gg in range(2):
                        g = 2 * gp + gg
                        rows = slice(gg * 64, (gg + 1) * 64)
                        for i in range(4):
                            j = tg * 4 + i
                            dms = [dm for dm in (-1, 0, 1) if 0 <= j + dm <= 15]
                            for k_, dm in enumerate(dms):
                                if dm == -1:
                                    v = 0
                                elif dm == 1:
                                    v = 1
                                else:
                                    v = 2 if j == 0 else (4 if j == 15 else 3)
                                pe.matmul(psp[rows, i * 128:(i + 1) * 128], lhsT=pvt[:, j + dm, g * 64:(g + 1) * 64],
                                          rhs=band(g, v), start=(k_ == 0), stop=(k_ == len(dms) - 1))
                    pld = pldr.next()
                    act.copy(out=pld, in_=psp[:, :])
                    plds[k] = pld

                def pool_B(k):
                    gp, tg = divmod(k, 4)
                    pld = plds.pop(k)
                    for gg in range(2):
                        rows = slice(gg * 64, (gg + 1) * 64)
                        psm = PS()
                        pe.matmul(psm[rows, :], lhsT=pw[rows, gp, :], rhs=pld[rows, :], start=True, stop=True)
                        dve.tensor_scalar(out=poolT[rows, gp, tsl(tg)], in0=psm[rows, :],
                                          scalar1=cst[rows, vo + 26 + gp:vo + 27 + gp], scalar2=None, op0=ALU.mult)
                pool_A(0)
                for k in range(8):
                    if k + 1 < 8:
                        pool_A(k + 1)
                    pool_B(k)
                work.release(m_p)
                sqr = Rot(work, 2, [TT], BF16)
                rsr = Rot(work, 2, [TT], F32)
                (wk,) = w_acquire(l, ["memk"])
                wkv = k8(wk, 256)

                def mk_proj(hp):
                    def f():
                        ps = PS()
                        for c in range(KC):
                            pe.matmul(ps[:, 0:256], lhsT=wkv[:, c, hp * 128:(hp + 1) * 128], rhs=memn[:, c, :],
                                      start=(c == 0), stop=(c == KC - 1))
                        return ps
                    return f
                qk_run([(mk_proj(hp), vo + 31, [(ALLR, mkT[:, hp, :])]) for hp in range(2)], sqr, rsr, n=256)
                (wv_,) = w_acquire(l, ["memv"])
                wvv = k8(wv_, 256)
                S.pool.memset(mvm, 0.0)
                for mt in range(2):
                    ps = PS()
                    for c in range(KC):
                        pe.matmul(ps[:, 0:256], lhsT=memn[:, c, mt * 128:(mt + 1) * 128], rhs=wvv[:, c, :],
                                  start=(c == 0), stop=(c == KC - 1))
                    for hh in range(2):
                        act.copy(out=mvm[:, mt, hh::2, hh * 64:(hh + 1) * 64],
                                 in_=ps[:, 0:256].rearrange("p (a b) -> p a b", a=2)[:, :, hh * 64:(hh + 1) * 64])
                work.release(mtop)
                mem_keep = work.mark()
                if stop_after == "memkv":
                    raise _Stop()

                if l == 0:
                    tap("pool", poolT)
                if stop_after == "pool":
                    raise _Stop()

                m_na = work.mark()
                mst = work.alloc([2, 2, 8, 2, 64], BF16)
                S.dma("pool", mst.rearrange("p a h s g q -> p (a h s g q)"), mst_d[l])
                gq8 = work.alloc([1], F32)
                dve.tensor_scalar(out=gq8, in0=cst[:, vo + 28:vo + 29], scalar1=0.125, scalar2=None, op0=ALU.mult)
                for hp in range(2):
                    m_hp = work.mark()
                    nqb = work.alloc([32, 2, 64], BF16)
                    nkT = work.alloc([S_LEN], BF16)
                    nvt = work.alloc([16, 128], BF16)
                    nvs = work.alloc([15, 128], BF16)
                    sqr = Rot(work, 2, [TT], BF16)
                    rsr = Rot(work, 2, [TT], F32)
                    S.pool.memset(nqb[R0, :, 1, :], 0.0)
                    S.pool.memset(nqb[R1, :, 0, :], 0.0)
                    (wn,) = w_acquire(l, [("nqk", hp)])
                    wnv = k8(wn, 256)
                    items = []
                    for tt in range(NT):
                        items.append(((lambda tt=tt: proj_fm(wnv, 0, tt)), gq8,
                                      [(R0, nqb[R0, tt * 8:(tt + 1) * 8, 0, :]), (R1, nqb[R1, tt * 8:(tt + 1) * 8, 1, :])]))
                    for tt in range(NT):
                        items.append(((lambda tt=tt: proj_fm(wnv, 128, tt)), vo + 29, [(ALLR, nkT[:, tsl(tt)])]))
                    qk_run(items, sqr, rsr)
                    (wnv_,) = w_acquire(l, [("nv", hp)])
                    wvv = k8(wnv_, 128)
                    for g4 in range(4):
                        ps = PS()
                        for i in range(4):
                            j = g4 * 4 + i
                            for c in range(KC):
                                pe.matmul(ps[:, i * 128:(i + 1) * 128], lhsT=hT[:, c, j * 128:(j + 1) * 128],
                                          rhs=wvv[:, c, :], start=(c == 0), stop=(c == KC - 1))
                        act.copy(out=nvt[:, g4 * 4:(g4 + 1) * 4, :],
                                 in_=ps[:, :].rearrange("p (a b) -> p a b", a=4))
                    S.dma("sp", nvs[0:64, :, :], nvt[64:128, 0:15, :])
                    S.dma("sp", nvs[64:128, :, :], nvt[0:64, 1:16, :])
                    ptr_ = Rot(work, 4, [4, 2, 64], BF16)
                    rcr = Rot(work, 2, [256], F32)
                    na_pT, na_acc = {}, {}

                    def na_A(r):
                        r0 = min(max(r - 4, 0), 24)
                        dl = r0 - r
                        par = dl & 1
                        s0 = (dl + 8) // 2 if par == 0 else (dl + 7) // 2
                        pss = PS()
                        pe.matmul(pss[:, :], lhsT=ident,
                                  rhs=mst[:, par, hp, s0:s0 + 4, :, :].rearrange("p s g q -> p (s g q)"),
                                  start=True, stop=False)
                        for i in range(4):
                            k0 = 64 * (r0 + 2 * i)
                            pe.matmul(pss[:, i * 128:(i + 1) * 128], lhsT=nkT[:, k0:k0 + 128],
                                      rhs=nqb[:, r, :, :].rearrange("p g q -> p (g q)"), start=False, stop=(i == 3))
                        pT = ptr_.next()
                        act.activation(out=pT.rearrange("p s g q -> p (s g q)"), in_=pss[:, :], func=AF.Exp)
                        na_pT[r] = pT

                    def na_B(r):
                        r0 = min(max(r - 4, 0), 24)
                        if r % 4 == 0:
                            na_acc[r // 4] = (PS(hold=True), PS(hold=True))
                        pso, psd = na_acc[r // 4]
                        pov = pso[:, :].rearrange("p (g r q) -> p g r q", g=2, r=4)
                        pdv = psd[:, :].rearrange("p (r g q) -> p r g q", r=4, g=2)
                        pT = na_pT.pop(r)
                        for hh in range(2):
                            for i in range(4):
                                if r0 % 2 == 0:
                                    vsrc = nvt[:, (r0 + 2 * i) // 2, :]
                                else:
                                    vsrc = nvs[:, (r0 - 1 + 2 * i) // 2, :]
                                pe.matmul(pov[:, hh, r % 4, :], lhsT=vsrc, rhs=pT[:, i, hh, :], start=(i == 0), stop=(i == 3))
                        for i in range(4):
                            pe.matmul(psd[:, (r % 4) * 128:(r % 4 + 1) * 128], lhsT=ones,
                                      rhs=pT[:, i, :, :].rearrange("p g q -> p (g q)"), start=(i == 0), stop=(i == 3))
                        if r % 4 == 3:
                            t0 = (r // 4) * 256
                            for hh, rows in ((0, R0), (1, R1)):
                                rc = rcr.next()
                                dve.reciprocal(out=rc[rows, :].rearrange("p (r q) -> p r q", r=4), in_=pdv[rows, :, hh, :])
                                dve.tensor_tensor(out=naT[rows, hp, t0:t0 + 256], in0=pov[rows, hh, :, :].rearrange("p r q -> p (r q)"),
                                                  in1=rc[rows, :], op=ALU.mult)
                            unhold(pso, psd)

                    na_A(0)
                    na_A(1)
                    for r in range(32):
                        if r + 2 < 32:
                            na_A(r + 2)
                        na_B(r)
                    work.release(m_hp)
                work.release(m_na)
                if l == 0:
                    tap("na", naT)
                if stop_after == "na":
                    raise _Stop()

                m_m = work.mark()
                (wm,) = w_acquire(l, ["mq"])
                wmv = k8(wm, 256)
                sqr = Rot(work, 2, [TT], BF16)
                rsr = Rot(work, 2, [TT], F32)
                ptr_ = Rot(work, 4, [2, TT], BF16)
                rcr = Rot(work, 2, [TT], F32)
                mqm = [work.alloc([S_LEN], BF16) for _ in range(2)]
                S.pool.memset(mqm[0][R1, :], 0.0)
                S.pool.memset(mqm[1][R0, :], 0.0)
                for hp in range(2):
                    qk_run([((lambda tt=tt: proj_fm(wmv, hp * 128, tt)), vo + 30,
                             [(R0, mqm[0][R0, tsl(tt)]), (R1, mqm[1][R1, tsl(tt)])]) for tt in range(NT)], sqr, rsr)
                    m_pT, m_acc = {}, {}

                    def m_A(k):
                        tt, hh = divmod(k, 2)
                        pT = ptr_.next()
                        for mt in range(2):
                            pss = PS()
                            pe.matmul(pss[:, :], lhsT=mkT[:, hp, mt * 128:(mt + 1) * 128], rhs=mqm[hh][:, tsl(tt)],
                                      start=True, stop=True)
                            act.activation(out=pT[:, mt, :], in_=pss[:, :], func=AF.Exp, scale=0.125)
                        m_pT[k] = pT

                    def m_B(k):
                        tt, hh = divmod(k, 2)
                        h = 2 * hp + hh
                        if hh == 0:
                            m_acc[tt] = (PS(hold=True), PS(hold=True))
                        pso, psd = m_acc[tt]
                        pT = m_pT.pop(k)
                        for mt in range(2):
                            pe.matmul(pso[:, :], lhsT=mvm[:, mt, h, :], rhs=pT[:, mt, :],
                                      start=(hh == 0 and mt == 0), stop=(hh == 1 and mt == 1))
                        for mt in range(2):
                            pe.matmul(psd[:, :], lhsT=hones[hh], rhs=pT[:, mt, :],
                                      start=(hh == 0 and mt == 0), stop=(hh == 1 and mt == 1))
                        if hh == 1:
                            rc = rcr.next()
                            dve.reciprocal(out=rc, in_=psd[:, :])
                            dve.tensor_tensor(out=moT[:, hp, tsl(tt)], in0=pso[:, :], in1=rc, op=ALU.mult)
                            unhold(pso, psd)

                    m_A(0)
                    m_A(1)
                    for k in range(2 * NT):
                        if k + 2 < 2 * NT:
                            m_A(k + 2)
                        m_B(k)
                work.release(m_m)
                if l == 0:
                    tap("mo", moT)
                if stop_after == "mo":
                    raise _Stop()
                work.release(ptop)

                mergedT = work.alloc([KC, S_LEN], BF16)
                m_g = work.mark()
                acc = work.alloc([NT, TT], F32)
                sgr = Rot(work, 2, [TT], F32)
                prr = Rot(work, 2, [TT], F32)
                for dc in range(8):
                    for np_ in range(2):
                        wg_, wb_ = w_acquire(l, [("gate", dc, np_), ("br", dc, np_)])
                        wgv = wg_[:, 0:2048].rearrange("p (k n c) -> p k n c", k=8, n=2)
                        wbv = wb_[:, 0:512].rearrange("p (k n c) -> p k n c", k=2, n=2)
                        for tt in range(NT):
                            for nn in range(2):
                                n_ = 2 * np_ + nn
                                pg = PS()
                                for c in range(KC):
                                    pe.matmul(pg[:, :], lhsT=wgv[:, c, nn, :], rhs=hT[:, c, tsl(tt)],
                                              start=(c == 0), stop=(c == KC - 1))
                                pu = PS()
                                for c in range(2):
                                    pe.matmul(pu[:, :], lhsT=wbv[:, c, nn, :], rhs=br[n_][:, c, tsl(tt)],
                                              start=(c == 0), stop=(c == 1))
                                sg = sgr.next()
                                act.activation(out=sg, in_=pg[:, :], func=AF.Sigmoid)
                                if n_ == 0:
                                    dve.tensor_tensor(out=acc[:, tt, :], in0=sg, in1=pu[:, :], op=ALU.mult)
                                else:
                                    pr = prr.next()
                                    dve.tensor_tensor(out=pr, in0=sg, in1=pu[:, :], op=ALU.mult)
                                    dst = mergedT[:, dc, tsl(tt)] if n_ == 3 else acc[:, tt, :]
                                    dve.tensor_tensor(out=dst, in0=acc[:, tt, :], in1=pr, op=ALU.add)
                work.release(m_g)
                if l == 0:
                    tap("merged", mergedT)
                f_sqr = Rot(work, 1, [KC, TT], BF16)
                f_rsr = Rot(work, 2, [TT], F32)
                f_s1, f_s2 = norm_make(vo + 16, f_sqr, f_rsr)
                wos = w_acquire(l, [("wout", j) for j in range(4)])
                wovs = [k8(w_, 256) for w_ in wos]

                def out_proj(tt):
                    for dc in range(8):
                        ps = proj_fm(wovs[dc // 2], (dc % 2) * 128, tt, src=mergedT)
                        dve.tensor_tensor(out=xT[:, dc, tsl(tt)], in0=xT[:, dc, tsl(tt)], in1=ps[:, :], op=ALU.add)
                out_proj(0)
                out_proj(1)
                f_s1(0)
                out_proj(2)
                f_s1(1)
                f_s2(0)
                out_proj(3)
                f_s1(2)
                f_s2(1)
                work.release(wtop)
                if l == 0:
                    tap("x1", xT)

                hid = work.alloc([FHALF, S_LEN], BF16)
                slr = Rot(work, 2, [TT], F32)
                def ffn1_tile(wfv, fi, tt):
                    pa = proj_fm(wfv, 0, tt)
                    pg = proj_fm(wfv, 128, tt)
                    sl = slr.next()
                    act.activation(out=sl, in_=pa[:, :], func=AF.Silu)
                    dve.tensor_tensor(out=hid[:, fi, tsl(tt)], in0=sl, in1=pg[:, :], op=ALU.mult)

                for half in range(2):
                    for fi in range(FHALF):
                        (wf,) = w_acquire(l, [("ffn1", half, fi)])
                        wfv = k8(wf, 256)
                        if half == 0 and fi == 0:
                            f_s1(3)
                            ffn1_tile(wfv, fi, 0)
                            f_s2(2)
                            ffn1_tile(wfv, fi, 1)
                            f_s2(3)
                            ffn1_tile(wfv, fi, 2)
                            ffn1_tile(wfv, fi, 3)
                            continue
                        for tt in range(NT):
                            ffn1_tile(wfv, fi, tt)
                    for dc in range(8):
                        (w2,) = w_acquire(l, [("ffn2", half, dc)])
                        w2v = w2[:, 0:FHALF * 128].rearrange("p (k c) -> p k c", k=FHALF)
                        for tt in range(NT):
                            ps = PS()
                            for fi in range(FHALF):
                                pe.matmul(ps[:, :], lhsT=w2v[:, fi, :], rhs=hid[:, fi, tsl(tt)],
                                          start=(fi == 0), stop=(fi == FHALF - 1))
                            dve.tensor_tensor(out=xT[:, dc, tsl(tt)], in0=xT[:, dc, tsl(tt)], in1=ps[:, :], op=ALU.add)
                work.release(wtop)

        except _Stop:
            pass
        fin = []
        for c in range(KC):
            S.dma("sp", outT_d[c * 128:(c + 1) * 128, :], xT[:, c, :])
            fin.append(S.last_dma)
        for name in tap_d:
            pass
        for key in list(S.sem.keys()):
            if isinstance(key, tuple) and key[-1] == "r":
                S.eng["sp"].wait_ge(S.sem[key], S.cnt[key])
        stats = dict(n_ins=S.n_ins, n_wait=S.n_wait)
    return nc, stats


_PROG_CACHE = {}


def _get_prog(nl, taps=()):
    key = (nl, tuple(t[0] for t in taps))
    if key not in _PROG_CACHE:
        _PROG_CACHE[key] = build_program(nl, taps)
    return _PROG_CACHE[key]


def pack_weights(inp, layers):
    plan1 = layer_plan()
    chunks = []
    for l in layers:
        for (_, n, parts) in plan1:
            chunks.append(pack_slab(inp, l, parts).reshape(-1))
    return np.concatenate(chunks)


def kernel(**inputs):
    inp = {k: np.asarray(v) for k, v in inputs.items()}
    x = inp["x"].astype(np.float32, copy=False)
    mem = inp["mem"].astype(np.float32, copy=False)
    B = x.shape[0]
    cst, cbf, rope, mst = _host_consts(inp)
    wts = pack_weights(inp, range(NL))
    nc, _ = _get_prog(NL)
    in_maps = []
    for b in range(B):
        in_maps.append({
            "xT": np.ascontiguousarray(x[b].T),
            "memT": np.ascontiguousarray(mem[b].T),
            "cst": cst, "cbf": cbf, "rope": rope, "namst": mst, "wts": wts,
        })
    res = run_bass_kernel_spmd(nc, in_maps, core_ids=list(range(B)))
    out = np.stack([np.ascontiguousarray(res.results[b]["outT"].T) for b in range(B)], axis=0)
    return out.astype(np.float32, copy=False)
```
